# Optimizing a Trainium2 kernel written in Bass

```python
import math
import jax, jax.numpy as jnp
from jax import lax
import numpy as np

D_MODEL = 1024
BATCH = 4
SEQ = 4096
DEPTH = 2

GRID_W = 64
N_MEM = 256
BRANCH_W = 256
N_BRANCH = 5
NA_HEADS = 4
NA_HD = 64
NA_KH = 8
NA_KW = 16
DA_HEADS = 4
DA_QK = 32
DA_V = 64
Q_BLOCK = 128
ALIBI_BASE = 8.0
POOL_WINDOWS = (2, 4, 8, 16)
POOL_GROUPS = 4
POOL_GC = 64
SG_CHUNK = 128
SG_GROUPS = 4
SG_GC = 64
MEM_HEADS = 4
MEM_HD = 64
N_SLABS = 15
SLAB_COLS = N_SLABS * BRANCH_W
IN_COLS = SLAB_COLS + N_BRANCH * D_MODEL
EPS = 1e-6

kernel_name = "gated_parallel_hybrid_encoder"


def rmsnorm(x, g):
    xf = x.astype(jnp.float32)
    y = xf * lax.rsqrt(jnp.mean(xf * xf, axis=-1, keepdims=True) + EPS)
    return (y * g.astype(jnp.float32)).astype(x.dtype)


def alibi_slopes(n_heads):
    return jnp.asarray(np.array([2.0 ** (-ALIBI_BASE * (h + 1) / n_heads) for h in range(n_heads)], dtype=np.float32))


def neighbourhood_attention(q, k, v, rpb):
    B, T, H, hd = q.shape
    R = T // GRID_W
    KH = min(NA_KH, R)
    rows = np.arange(R)
    rs = np.clip(rows - KH // 2, 0, R - KH)
    row_idx = rs[:, None] + np.arange(KH)[None, :]
    cols = np.arange(GRID_W)
    cs = np.clip(cols - NA_KW // 2, 0, GRID_W - NA_KW)
    col_mask = (cols[None, :] >= cs[:, None]) & (cols[None, :] < cs[:, None] + NA_KW)
    roff = row_idx - rows[:, None] + NA_KH - 1
    coff = np.clip(cols[None, :] - cols[:, None], -(NA_KW - 1), NA_KW - 1) + NA_KW - 1
    bias = rpb.astype(jnp.float32)[:, roff[:, :, None, None], coff[None, None, :, :]]
    bias = bias.transpose(0, 1, 3, 2, 4)
    qg = q.reshape(B, R, GRID_W, H, hd)
    kg = k.reshape(B, R, GRID_W, H, hd)[:, row_idx]
    vg = v.reshape(B, R, GRID_W, H, hd)[:, row_idx]
    s = jnp.einsum('brqhd,brikhd->bhrqik', qg, kg).astype(jnp.float32) * (hd ** -0.5)
    s = s + bias[None]
    s = jnp.where(jnp.asarray(col_mask)[:, None, :], s, jnp.finfo(jnp.float32).min)
    p = jax.nn.softmax(s.reshape(B, H, R, GRID_W, KH * GRID_W), axis=-1)
    p = p.reshape(B, H, R, GRID_W, KH, GRID_W)
    o = jnp.einsum('bhrqik,brikhd->brqhd', p, vg.astype(jnp.float32))
    return o.reshape(B, T, H, hd).astype(q.dtype)


def diff_attention(q, k, v, lam):
    B, T, H, _, d = q.shape
    nb = T // Q_BLOCK
    slopes = alibi_slopes(H)
    qb = q.reshape(B, nb, Q_BLOCK, H, 2, d).transpose(1, 0, 2, 3, 4, 5)
    kpos = jnp.arange(T)
    vf = v.astype(jnp.float32)

    def block(args):
        qi, start = args
        s = jnp.einsum('bqhmd,bkhmd->bhmqk', qi, k).astype(jnp.float32) * (d ** -0.5)
        qpos = start + jnp.arange(Q_BLOCK)
        dist = jnp.abs(qpos[:, None] - kpos[None, :]).astype(jnp.float32)
        s = s - slopes[None, :, None, None, None] * dist
        p = jax.nn.softmax(s, axis=-1)
        a = p[:, :, 0] - lam * p[:, :, 1]
        return jnp.einsum('bhqk,bkhe->bqhe', a, vf)

    out = lax.map(block, (qb, jnp.arange(nb) * Q_BLOCK))
    return out.transpose(1, 0, 2, 3, 4).reshape(B, T, H, v.shape[-1]).astype(q.dtype)


def multiscale_pool_mixer(xc, c_w, c_scale):
    B, T, C = xc.shape
    xf = xc.astype(jnp.float32)
    csum = jnp.concatenate([jnp.zeros((B, 1, C), jnp.float32), jnp.cumsum(xf, axis=1)], axis=1)
    t = np.arange(T)
    outs = []
    for g, w in enumerate(POOL_WINDOWS):
        lo = np.clip(t - w // 2, 0, T - 1)
        hi = np.clip(t - w // 2 + w - 1, 0, T - 1)
        cnt = jnp.asarray((hi - lo + 1).astype(np.float32))
        sl = slice(g * POOL_GC, (g + 1) * POOL_GC)
        wsum = csum[:, hi + 1, sl] - csum[:, lo, sl]
        outs.append(wsum / cnt[None, :, None] - xf[:, :, sl])
    dlt = jnp.stack(outs, axis=2)
    y = jnp.einsum('btgc,gce->btge', dlt, c_w.astype(jnp.float32)).reshape(B, T, C)
    return (y * c_scale.astype(jnp.float32)).astype(xc.dtype)


def spatial_gating(u, v, ln_g, ln_b, ws, bs):
    B, T, C = v.shape
    vf = v.astype(jnp.float32)
    mu = jnp.mean(vf, axis=-1, keepdims=True)
    var = jnp.mean((vf - mu) ** 2, axis=-1, keepdims=True)
    vn = (vf - mu) * lax.rsqrt(var + EPS) * ln_g.astype(jnp.float32) + ln_b.astype(jnp.float32)
    vc = vn.reshape(B, T // SG_CHUNK, SG_CHUNK, SG_GROUPS, SG_GC)
    mixed = jnp.einsum('gps,bnsgc->bnpgc', ws.astype(jnp.float32), vc)
    mixed = mixed + bs.astype(jnp.float32).T[None, None, :, :, None]
    return (u.astype(jnp.float32) * mixed.reshape(B, T, C)).astype(u.dtype)


def memory_attention(q, k, v):
    s = jnp.einsum('bthd,bmhd->bhtm', q, k).astype(jnp.float32) * (q.shape[-1] ** -0.5)
    p = jax.nn.softmax(s, axis=-1)
    return jnp.einsum('bhtm,bmhd->bthd', p, v.astype(jnp.float32)).astype(q.dtype)


def hybrid_layer(x, mem, layer_idx, norm_g, w_in, b_gate, a_qn_g, a_kn_g, a_rpb,
                 b_qn_g, b_kn_g, b_lam_q1, b_lam_k1, b_lam_q2, b_lam_k2, b_sub_g,
                 c_w, c_scale, d_ln_g, d_ln_b, d_ws, d_bs,
                 m_norm_g, m_wkv, m_qn_g, m_kn_g, w_branch, w_out):
    B, T, _ = x.shape
    h = rmsnorm(x, norm_g)
    proj = h @ w_in
    (a_q, a_k, a_v, a_z, b_q, b_k, b_v, b_z, c_x, c_z,
     d_u, d_v, d_z, m_q, m_z) = jnp.split(proj[..., :SLAB_COLS], N_SLABS, axis=-1)

    qa = rmsnorm(a_q.reshape(B, T, NA_HEADS, NA_HD), a_qn_g)
    ka = rmsnorm(a_k.reshape(B, T, NA_HEADS, NA_HD), a_kn_g)
    va = a_v.reshape(B, T, NA_HEADS, NA_HD)
    y_a = neighbourhood_attention(qa, ka, va, a_rpb).reshape(B, T, BRANCH_W) * jax.nn.silu(a_z)

    qb = rmsnorm(b_q.reshape(B, T, DA_HEADS, 2, DA_QK), b_qn_g)
    kb = rmsnorm(b_k.reshape(B, T, DA_HEADS, 2, DA_QK), b_kn_g)
    vb = b_v.reshape(B, T, DA_HEADS, DA_V)
    lam_init = 0.8 - 0.6 * math.exp(-0.3 * layer_idx)
    lam = (jnp.exp(jnp.sum(b_lam_q1.astype(jnp.float32) * b_lam_k1.astype(jnp.float32)))
           - jnp.exp(jnp.sum(b_lam_q2.astype(jnp.float32) * b_lam_k2.astype(jnp.float32))) + lam_init)
    ob = diff_attention(qb, kb, vb, lam)
    ob = rmsnorm(ob, b_sub_g) * (1.0 - lam_init)
    y_b = ob.reshape(B, T, BRANCH_W) * jax.nn.silu(b_z)

    y_c = multiscale_pool_mixer(c_x, c_w, c_scale) * jax.nn.silu(c_z)

    y_d = spatial_gating(jax.nn.gelu(d_u), jax.nn.gelu(d_v), d_ln_g, d_ln_b, d_ws, d_bs) * jax.nn.silu(d_z)

    mkv = rmsnorm(mem, m_norm_g) @ m_wkv
    mk, mv = jnp.split(mkv, 2, axis=-1)
    M = mem.shape[1]
    qm = rmsnorm(m_q.reshape(B, T, MEM_HEADS, MEM_HD), m_qn_g)
    km = rmsnorm(mk.reshape(B, M, MEM_HEADS, MEM_HD), m_kn_g)
    vm = mv.reshape(B, M, MEM_HEADS, MEM_HD)
    y_m = memory_attention(qm, km, vm).reshape(B, T, BRANCH_W) * jax.nn.silu(m_z)

    merged = jnp.zeros_like(x)
    for i, y in enumerate((y_a, y_b, y_c, y_d, y_m)):
        g_logit = proj[..., SLAB_COLS + i * D_MODEL: SLAB_COLS + (i + 1) * D_MODEL] + b_gate[i]
        merged = merged + jax.nn.sigmoid(g_logit.astype(jnp.float32)).astype(x.dtype) * (y @ w_branch[i])
    return x + merged @ w_out


def setup_inputs(seed: int = 0) -> dict:
    key = jax.random.key(seed)
    ks = jax.random.split(key, 32)
    f32 = jnp.float32
    L, D = DEPTH, D_MODEL

    def nrm(k, shape, scale):
        return jax.random.normal(k, shape, f32) * scale

    return {
        "x": nrm(ks[0], (BATCH, SEQ, D), 1.0),
        "mem": nrm(ks[1], (BATCH, N_MEM, D), 1.0),
        "norm_g": 1.0 + nrm(ks[2], (L, D), 0.05),
        "w_in": nrm(ks[3], (L, D, IN_COLS), D ** -0.5),
        "b_gate": nrm(ks[4], (L, N_BRANCH, D), 0.01),
        "a_qn_g": 1.0 + nrm(ks[5], (L, NA_HD), 0.05),
        "a_kn_g": 1.0 + nrm(ks[6], (L, NA_HD), 0.05),
        "a_rpb": nrm(ks[7], (L, NA_HEADS, 2 * NA_KH - 1, 2 * NA_KW - 1), 0.1),
        "b_qn_g": 1.0 + nrm(ks[8], (L, DA_QK), 0.05),
        "b_kn_g": 1.0 + nrm(ks[9], (L, DA_QK), 0.05),
        "b_lam_q1": nrm(ks[10], (L, DA_QK), 0.1),
        "b_lam_k1": nrm(ks[11], (L, DA_QK), 0.1),
        "b_lam_q2": nrm(ks[12], (L, DA_QK), 0.1),
        "b_lam_k2": nrm(ks[13], (L, DA_QK), 0.1),
        "b_sub_g": 1.0 + nrm(ks[14], (L, DA_V), 0.05),
        "c_w": nrm(ks[15], (L, POOL_GROUPS, POOL_GC, POOL_GC), POOL_GC ** -0.5),
        "c_scale": 1.0 + nrm(ks[16], (L, POOL_GROUPS * POOL_GC), 0.1),
        "d_ln_g": 1.0 + nrm(ks[17], (L, SG_GROUPS * SG_GC), 0.05),
        "d_ln_b": nrm(ks[18], (L, SG_GROUPS * SG_GC), 0.02),
        "d_ws": nrm(ks[19], (L, SG_GROUPS, SG_CHUNK, SG_CHUNK), SG_CHUNK ** -0.5),
        "d_bs": 1.0 + nrm(ks[20], (L, SG_GROUPS, SG_CHUNK), 0.01),
        "m_norm_g": 1.0 + nrm(ks[21], (L, D), 0.05),
        "m_wkv": nrm(ks[22], (L, D, 2 * BRANCH_W), D ** -0.5),
        "m_qn_g": 1.0 + nrm(ks[23], (L, MEM_HD), 0.05),
        "m_kn_g": 1.0 + nrm(ks[24], (L, MEM_HD), 0.05),
        "w_branch": nrm(ks[25], (L, N_BRANCH, BRANCH_W, D), BRANCH_W ** -0.5),
        "w_out": nrm(ks[26], (L, D, D), (2.0 * D) ** -0.5),
    }


def reference(x, mem, norm_g, w_in, b_gate, a_qn_g, a_kn_g, a_rpb,
              b_qn_g, b_kn_g, b_lam_q1, b_lam_k1, b_lam_q2, b_lam_k2, b_sub_g,
              c_w, c_scale, d_ln_g, d_ln_b, d_ws, d_bs,
              m_norm_g, m_wkv, m_qn_g, m_kn_g, w_branch, w_out):
    for l in range(DEPTH):
        x = hybrid_layer(x, mem, l, norm_g[l], w_in[l], b_gate[l], a_qn_g[l], a_kn_g[l], a_rpb[l],
                         b_qn_g[l], b_kn_g[l], b_lam_q1[l], b_lam_k1[l], b_lam_q2[l], b_lam_k2[l], b_sub_g[l],
                         c_w[l], c_scale[l], d_ln_g[l], d_ln_b[l], d_ws[l], d_bs[l],
                         m_norm_g[l], m_wkv[l], m_qn_g[l], m_kn_g[l], w_branch[l], w_out[l])
    return x
```

```python
import math
from contextlib import ExitStack

import numpy as np
import ml_dtypes

import concourse.bass as bass
import concourse.mybir as mybir
from concourse.bass_utils import run_bass_kernel_spmd

F32 = mybir.dt.float32
BF16 = mybir.dt.bfloat16
AF = mybir.ActivationFunctionType
ALU = mybir.AluOpType
AX = mybir.AxisListType

EPS = 1e-6
NEG = -30000.0
NT = 2048
NTT = 16
D = 1024
SLOPES = [2.0 ** (-8.0 * (h + 1) / 4) for h in range(4)]

C_AQ, C_AK, C_AV, C_AZ = 0, 256, 512, 768
C_BQ, C_BK, C_BV, C_BZ = 1024, 1280, 1536, 1792
C_CX, C_CZ = 2048, 2304
C_DU, C_DV, C_DZ = 2560, 2816, 3072
C_MQ, C_MZ = 3328, 3584
C_GATE = 3840

COMPUTE = ("pe", "act", "dve", "pool")
QUEUES = ("pe", "act", "dve", "pool", "sp")
EPOCH = 16000
N_DMA_SEMS = 24


class _Op:
    __slots__ = ("eng", "fn", "deps", "is_dma", "signal", "tok", "idx", "ndma", "dsem")

    def __init__(self, eng, fn, is_dma):
        self.eng = eng
        self.fn = fn
        self.deps = []
        self.is_dma = is_dma
        self.signal = False
        self.tok = None
        self.ndma = 0
        self.dsem = None


class Sched:
    def __init__(self, nc, es, same_engine_sync=True):
        self.nc = nc
        self.es = es
        self.ops = []
        self.last_w = {}
        self.readers = {}
        self.same_engine_sync = same_engine_sync
        import os as _os
        nsp = int(_os.environ.get("NSP", "3"))
        npl = int(_os.environ.get("NPL", "3"))
        self.dma_pools = {"sp": list(range(0, nsp)), "pool": list(range(14, 14 + npl)), "act": list(range(22, 24))}
        self.dma_rr = {q: 0 for q in self.dma_pools}
        self.dma_last = [None] * N_DMA_SEMS
        self.sb_off = {}
        self.sb_hi = 0

    def arena(self, name, base):
        self.sb_off[name] = base

    def sb(self, name, shape, dt, arena="main"):
        nbytes = int(np.prod(shape[1:])) * (2 if dt == BF16 else 4)
        nbytes = (nbytes + 63) // 64 * 64
        off = self.sb_off[arena]
        self.sb_off[arena] = off + nbytes
        self.sb_hi = max(self.sb_hi, off + nbytes)
        return self.nc.alloc_sbuf_tensor_at(name, list(shape), dt, offset=off)

    def ps(self, name, shape, dt):
        return self.es.enter_context(self.nc.psum_tensor(name, list(shape), dt))

    def _add(self, op, r, w):
        deps = set()
        for k in r:
            lw = self.last_w.get(k)
            if lw is not None:
                deps.add(lw)
        for k in w:
            lw = self.last_w.get(k)
            if lw is not None:
                deps.add(lw)
            for rd in self.readers.get(k, ()):
                deps.add(rd)
        deps.discard(op)
        op.deps = sorted(deps, key=lambda o: o.idx)
        for k in r:
            self.readers.setdefault(k, []).append(op)
        for k in w:
            self.last_w[k] = op
            self.readers[k] = []
        return op

    def op(self, eng, fn, r=(), w=()):
        o = _Op(eng, fn, False)
        o.idx = len(self.ops)
        self.ops.append(o)
        return self._add(o, r, w)

    def dma(self, queue, out, in_, r=(), w=(), **kw):
        def fn(e, out=out, in_=in_, kw=kw):
            return [e.dma_start(out=out, in_=in_, **kw)]
        return self.dma_multi(queue, fn, 1, r, w)

    def dma_multi(self, queue, fn, n, r=(), w=()):
        o = _Op(queue, fn, True)
        o.idx = len(self.ops)
        o.ndma = n
        self.ops.append(o)
        pool_ = self.dma_pools[queue]
        s = pool_[self.dma_rr[queue] % len(pool_)]
        self.dma_rr[queue] += 1
        o.dsem = s
        self._add(o, r, w)
        prev = self.dma_last[s]
        if prev is not None and prev not in o.deps:
            o.deps.append(prev)
        self.dma_last[s] = o
        return o

    def barrier(self):
        lasts = {}
        for o in self.ops:
            if o.is_dma:
                lasts[("d", o.dsem)] = o
            elif o.fn is not None and o.eng in COMPUTE:
                lasts[("c", o.eng)] = o
        deps = sorted(lasts.values(), key=lambda o: o.idx)
        for q in QUEUES:
            o = _Op(q, (lambda e: e.nop()), False)
            o.idx = len(self.ops)
            o.deps = [dd for dd in deps if dd.is_dma or dd.eng != q]
            self.ops.append(o)

    def finish(self, outputs=()):
        nc = self.nc
        fin = _Op("sp", None, False)
        fin.idx = len(self.ops)
        fin.deps = [self.last_w[k] for k in outputs]
        self.ops.append(fin)
        for o in self.ops:
            for d in o.deps:
                if not d.is_dma:
                    if d.eng == o.eng and (d.eng == "pe" or not self.same_engine_sync):
                        continue
                    d.signal = True
        counts = {e: 0 for e in COMPUTE}
        for o in self.ops:
            if o.is_dma or o.fn is None:
                continue
            if o.signal:
                counts[o.eng] += 1
                o.tok = ("c", o.eng, counts[o.eng])
        dcount = [0] * N_DMA_SEMS
        for o in self.ops:
            if o.is_dma:
                dcount[o.dsem] += 16 * o.ndma
                o.tok = ("d", o.dsem, dcount[o.dsem])
        es = self.es
        csem = {}
        for e in COMPUTE:
            n_ep = counts[e] // EPOCH + 1
            csem[e] = [es.enter_context(nc.semaphore(f"s_{e}_{i}")) for i in range(n_ep)]
        dsem = [es.enter_context(nc.semaphore(f"s_dma_{i}")) for i in range(N_DMA_SEMS)]

        def resolve(tok):
            if tok[0] == "c":
                ep = (tok[2] - 1) // EPOCH
                return csem[tok[1]][ep], tok[2] - ep * EPOCH, ("c", tok[1], ep)
            return dsem[tok[1]], tok[2], ("d", tok[1])

        streams = {q: [o for o in self.ops if o.eng == q] for q in QUEUES}
        stats = {q: [len(streams[q]), 0] for q in QUEUES}

        def emit(q, e):
            waited = {}
            for o in streams[q]:
                for d in o.deps:
                    if d.tok is None:
                        continue
                    if (not d.is_dma) and d.eng == q and (q == "pe" or not self.same_engine_sync):
                        continue
                    sem, val, key = resolve(d.tok)
                    if waited.get(key, 0) >= val:
                        continue
                    if key[0] == "c":
                        skip = False
                        for k2 in waited:
                            if k2[0] == "c" and k2[1] == key[1] and k2[2] > key[2]:
                                skip = True
                        if skip:
                            continue
                    e.wait_ge(sem, val)
                    stats[q][1] += 1
                    waited[key] = val
                if o.fn is None:
                    continue
                if o.is_dma:
                    insts = o.fn(e)
                    assert len(insts) == o.ndma, (len(insts), o.ndma)
                    for i in insts:
                        i.then_inc(dsem[o.dsem], 16)
                else:
                    inst = o.fn(e)
                    if o.signal:
                        sem, _, _ = resolve(o.tok)
                        inst.then_inc(sem, 1)

        with nc.Block() as block:
            @block.sync
            def _(e):
                emit("sp", e)

            @block.tensor
            def _(e):
                emit("pe", e)

            @block.scalar
            def _(e):
                emit("act", e)

            @block.vector
            def _(e):
                emit("dve", e)

            @block.gpsimd
            def _(e):
                emit("pool", e)
        self.stats = stats
        return stats


A_D_LIST = [-3, -2, -1, 0, 1, 2, 3]


def a_slots(t):
    if t == 0:
        own = [0, 1, 2, 3]
        halo = [(16, -2), (17, -1)]
    elif t == 1:
        own = [0, 1, 2, 3]
        halo = [(17, -2)]
    elif t == 14:
        own = [12, 13, 14, 15]
        halo = [(18, 2)]
    elif t == 15:
        own = [12, 13, 14, 15]
        halo = [(18, 1), (19, 2)]
    else:
        own = list(range(t - 2, t + 3))
        halo = []
    out = [(s, s - t) for s in own] + halo
    return [(s, A_D_LIST.index(d)) for s, d in out]


A_MAXSLOT = 6


def host_consts(hf):
    c = {}
    c["ident"] = np.eye(128, dtype=np.float32)
    b64 = np.zeros((128, 128), np.float32)
    b64[:64, :64] = 1
    b64[64:, 64:] = 1
    c["bones64"] = b64
    b32 = np.zeros((64, 64), np.float32)
    b32[:32, :32] = 1
    b32[32:, 32:] = 1
    c["bones32"] = b32
    i = np.arange(128)[:, None]
    j = np.arange(512)[None, :]
    c["U"] = (j - i).astype(np.float32)
    m = np.arange(896)[None, :]
    c["Rt"] = np.maximum(i - m + 384, 0).astype(np.float32)
    scB = np.zeros((4, 32, 4), np.float32)
    biasB = np.zeros((4, 32, 4), np.float32)
    sc2B = np.zeros((4, 8, 4), np.float32)
    for qb in range(4):
        Q0 = hf * 2048 + qb * 512
        for kt in range(32):
            K0 = kt * 128
            Dd = Q0 - K0
            for h in range(4):
                sl = SLOPES[h]
                if K0 + 127 < Q0 or (Q0 <= K0 < Q0 + 512):
                    scB[qb, kt, h] = -sl
                    biasB[qb, kt, h] = -sl * Dd
                else:
                    scB[qb, kt, h] = sl
                    biasB[qb, kt, h] = sl * Dd
        for ci, kt in enumerate(b_cands(qb)):
            K0 = kt * 128
            if Q0 <= K0 < Q0 + 512:
                for h in range(4):
                    sc2B[qb, ci, h] = -2.0 * SLOPES[h]
    c["scB"] = np.broadcast_to(scB.reshape(1, 512), (128, 512)).copy()
    c["biasB"] = np.broadcast_to(biasB.reshape(1, 512), (128, 512)).copy()
    c["sc2B"] = np.broadcast_to(sc2B.reshape(1, 128), (128, 128)).copy()
    rv = np.zeros((128, NTT, A_MAXSLOT, 2), np.float32)
    for t in range(NTT):
        R0 = 32 * hf + 2 * t
        for si, (slot, _) in enumerate(a_slots(t)):
            if slot < 16:
                pair = 16 * hf + slot
            else:
                pair = 14 + (slot - 16)
            own_dup = (slot >= 16) and (pair // 16 == hf)
            for qr in range(2):
                r = R0 + qr
                rs = min(max(r - 4, 0), 56)
                for kr in range(2):
                    rp = 2 * pair + kr
                    ok = (rs <= rp < rs + 8) and not own_dup
                    rv[kr * 64:(kr + 1) * 64, t, si, qr] = 0.0 if ok else NEG
    c["rowvalid"] = rv.reshape(128, NTT * A_MAXSLOT * 2)
    T = 4096
    pw = np.zeros((128, 7, 4, 128), np.float32)

    def pwmat(src0, dst0, w):
        M = np.zeros((128, 128), np.float64)
        for tt in range(128):
            t = dst0 + tt
            lo = min(max(t - w // 2, 0), T - 1)
            hi = min(max(t - w // 2 + w - 1, 0), T - 1)
            cnt = hi - lo + 1
            for s in range(lo, hi + 1):
                if src0 <= s < src0 + 128:
                    M[s - src0, tt] += 1.0 / cnt
            if src0 <= t < src0 + 128:
                M[t - src0, tt] -= 1.0
        return M.astype(np.float32)

    base = hf * 2048
    for g, w in enumerate((2, 4, 8, 16)):
        mid = 1024
        pw[:, 0, g] = pwmat(mid - 128, mid, w)
        pw[:, 1, g] = pwmat(mid, mid, w)
        pw[:, 2, g] = pwmat(mid + 128, mid, w)
        pw[:, 3, g] = pwmat(base - 128, base, w) if hf == 1 else 0.0
        pw[:, 4, g] = pwmat(base, base, w)
        pw[:, 5, g] = pwmat(base + 1920, base + 1920, w)
        pw[:, 6, g] = pwmat(base + 2048, base + 1920, w) if hf == 0 else 0.0
    c["pwT"] = pw.reshape(128, 7 * 4 * 128)
    return c


CPACK = [("ident", 128), ("bones64", 128), ("bones32", 64), ("U", 512), ("Rt", 896), ("scB", 512),
         ("biasB", 512), ("sc2B", 128), ("rowvalid", NTT * A_MAXSLOT * 2), ("pwT", 3584)]
CPACK_W = sum(w_ for _, w_ in CPACK)


def pack_consts(c):
    out = np.zeros((128, CPACK_W), np.float32)
    off = 0
    for nm, w_ in CPACK:
        a = c[nm]
        out[:a.shape[0], off:off + w_] = a
        off += w_
    return out


def b_cands(qb):
    return [4 * qb + k for k in range(4)] + [16 + 4 * qb + k for k in range(4)]


def host_layer_tables(inp, l):
    t = {}
    t["rowp"] = np.concatenate([inp["norm_g"][l], inp["m_norm_g"][l], inp["d_ln_g"][l],
                                inp["d_ln_b"][l]]).astype(np.float32)[None, :]
    colp = np.zeros((128, 64), np.float32)
    colp[:, 0:40] = inp["b_gate"][l].reshape(5, 8, 128).transpose(2, 0, 1).reshape(128, 40)
    p = np.arange(128)
    colp[:, 40] = inp["a_qn_g"][l][p % 64]
    colp[:, 41] = inp["a_kn_g"][l][p % 64]
    colp[:, 42] = inp["m_qn_g"][l][p % 64]
    colp[:, 43] = inp["m_kn_g"][l][p % 64]
    colp[:, 44] = inp["b_qn_g"][l][p % 32]
    colp[:, 45] = inp["b_kn_g"][l][p % 32]
    for g in range(4):
        colp[:, 46 + g] = inp["c_scale"][l][g * 64 + p % 64]
        colp[:, 51 + g] = inp["d_bs"][l][g]
    colp[:, 50] = inp["b_sub_g"][l][p % 64]
    lo_ = (p < 64)
    colp[:, 58] = np.where(lo_, inp["a_qn_g"][l][p % 64], 0.0)
    colp[:, 59] = np.where(~lo_, inp["a_qn_g"][l][p % 64], 0.0)
    colp[:, 60] = np.where(lo_, inp["m_qn_g"][l][p % 64], 0.0)
    colp[:, 61] = np.where(~lo_, inp["m_qn_g"][l][p % 64], 0.0)
    li = 0.8 - 0.6 * math.exp(-0.3 * l)
    colp[:, 55] = 1.0 / (64.0 * (1.0 - li) ** 2)
    colp[:, 56] = EPS / (1.0 - li) ** 2
    colp[:, 57] = li
    t["colp"] = colp
    t["lamp"] = np.concatenate([inp["b_lam_q1"][l], inp["b_lam_k1"][l], inp["b_lam_q2"][l],
                                inp["b_lam_k2"][l]]).astype(np.float32)[None, :]
    rpb = inp["a_rpb"][l]
    cols = np.arange(64)
    cs = np.clip(cols - 8, 0, 48)
    colok = (cols[:, None] >= cs[None, :]) & (cols[:, None] < cs[None, :] + 16)
    coff = np.clip(cols[:, None] - cols[None, :], -15, 15) + 15
    bval = np.full((128, 7, 4, 128), NEG, np.float32)
    for di, d in enumerate(A_D_LIST):
        for kr in range(2):
            for qr in range(2):
                dr = 2 * d + kr - qr
                if abs(dr) > 7:
                    continue
                for h in range(4):
                    blk = np.where(colok, rpb[h, dr + 7][coff], np.float32(NEG))
                    bval[kr * 64:(kr + 1) * 64, di, h, qr * 64:(qr + 1) * 64] = blk
    t["bval"] = bval.reshape(128, 7 * 4 * 128)
    t["dwsT"] = np.ascontiguousarray(inp["d_ws"][l].transpose(2, 0, 1)).reshape(128, 512)
    t["cw"] = np.ascontiguousarray(inp["c_w"][l].transpose(1, 0, 2)).reshape(64, 256)
    return t


class Prog:
    def __init__(self, mode, nl, debug=False):
        self.mode = mode
        self.nl = nl
        self.debug = debug
        self.nc = bass.Bass("TRN2", target_bir_lowering=False)
        self.es = ExitStack()
        self.S = Sched(self.nc, self.es)
        self.build()

    def din(self, name, shape, dt=F32):
        return self.nc.dram_tensor(name, list(shape), dt, kind="ExternalInput").ap()

    def dout(self, name, shape, dt=F32):
        return self.nc.dram_tensor(name, list(shape), dt, kind="ExternalOutput").ap()

    def dint(self, name, shape, dt=F32):
        return self.nc.dram_tensor(name, list(shape), dt, kind="Internal").ap()

    def build(self):
        nc, S, nl, mode = self.nc, self.S, self.nl, self.mode
        d = self.d = {}
        d["x"] = self.din("x", [NT, D])
        d["mem"] = self.din("mem", [256, D])
        d["w_in"] = self.din("w_in", [nl, D, 8960])
        d["m_wkv"] = self.din("m_wkv", [nl, D, 512])
        d["w_branch"] = self.din("w_branch", [nl, 5, 256, D])
        d["w_out"] = self.din("w_out", [nl, D, D])
        d["rowp"] = self.din("rowp", [nl, 1, 2560])
        d["colp"] = self.din("colp", [nl, 128, 64])
        d["lamp"] = self.din("lamp", [nl, 1, 128])
        d["bval"] = self.din("bval", [nl, 128, 3584])
        d["dwsT"] = self.din("dwsT", [nl, 128, 512])
        d["cw"] = self.din("cw", [nl, 64, 256])
        cpack = self.din("cpack", [128, CPACK_W])
        off = 0
        for nm, w_ in CPACK:
            d[nm] = cpack[0:64, off:off + w_] if nm == "bones32" else cpack[:, off:off + w_]
            off += w_
        if mode == "P":
            d["kt_own"] = self.dout("kt_own", [512, NT], BF16)
            d["v_own"] = self.dout("v_own", [NT, 768], BF16)
        elif mode in ("M", "MB", "MG"):
            d["kt_own"] = self.dint("kt_own", [512, NT], BF16)
            d["v_own"] = self.dint("v_own", [NT, 768], BF16)
            d["kt_all"] = self.din("kt_all", [2, 512, NT], BF16)
            d["v_all"] = self.din("v_all", [2, NT, 768], BF16)
            if mode == "MG":
                d["obn_i"] = self.din("obn_i", [128, NTT * 256], BF16)
            if mode == "MB":
                d["obn_o"] = self.dout("obn_o", [128, NTT * 256], BF16)
            else:
                d["xo"] = self.dout("xo", [NT, D])
        else:
            d["kt_own"] = self.dint("kt_own", [512, NT], BF16)
            d["v_own"] = self.dint("v_own", [NT, 768], BF16)
            d["kt_all"] = self.dint("kt_all", [2, 512, NT], BF16)
            d["v_all"] = self.dint("v_all", [2, NT, 768], BF16)
            d["xo"] = self.dout("xo", [NT, D])
        if self.debug:
            d["dbg"] = self.dout("dbg", [5, 256, NT])

        base = 16512
        S.arena("main", base)
        sb = S.sb
        t = self.t = {}
        t["x"] = sb("x", [128, NTT, D], F32)
        t["hT"] = sb("hT", [128, 8, 512], BF16)
        t["obn"] = sb("obn", [128, NTT, 256], BF16)
        t["mkT"] = sb("mkT", [128, 2, 256], BF16)
        t["mv"] = sb("mv", [128, 2, 4, 65], BF16)
        t["ident"] = sb("ident", [128, 128], BF16)
        t["bones64"] = sb("bones64", [128, 128], BF16)
        t["bones32"] = sb("bones32", [64, 64], BF16)
        t["colp"] = sb("colp", [128, 64], F32)
        t["gbc"] = sb("gbc", [128, D], F32)
        t["neglam"] = sb("neglam", [128, 1], F32)
        t["ss4"] = sb("ss4", [128, 4], F32)
        t["hb"] = sb("hb", [128, D], BF16)
        t["sq"] = sb("sq", [128, 512], BF16)
        t["std"] = sb("std", [128, 512], F32)
        NRING = 3
        t["ring"] = [sb(f"ring{i}", [128, 2048], BF16) for i in range(NRING)]
        common_end = S.sb_off["main"]
        S.arena("PB", common_end)
        t["bqT"] = sb("bqT", [64, 4, NT], BF16, "PB")
        t["bkT"] = [sb(f"bkT{i}", [64, 4096], BF16, "PB") for i in range(2)]
        t["bv"] = [sb(f"bv{i}", [128, 32, 65], BF16, "PB") for i in range(2)]
        t["U"] = sb("U", [128, 512], F32, "PB")
        t["Rt"] = sb("Rt", [128, 896], F32, "PB")
        t["scB"] = sb("scB", [128, 512], F32, "PB")
        t["biasB"] = sb("biasB", [128, 512], F32, "PB")
        t["sc2B"] = sb("sc2B", [128, 128], F32, "PB")
        t["btmp"] = [sb(f"btmp{i}", [128, 512], F32, "PB") for i in range(2)]
        t["bpt"] = [sb(f"bpt{i}", [128, 512], BF16, "PB") for i in range(3)]
        t["bo"] = sb("bo", [128, 2, 4, 64], F32, "PB")
        t["bob"] = sb("bob", [128, 4, 64], F32, "PB")
        t["bsq"] = sb("bsq", [128, 4, 64], F32, "PB")
        t["brd"] = sb("brd", [128, 4], F32, "PB")
        t["vtok"] = sb("vtok", [128, 768], BF16, "PB")
        t["ktmp"] = sb("ktmp", [128, 512], BF16, "PB")
        t["hmT"] = sb("hmT", [128, 8, 256], BF16, "PB")
        t["mgbc"] = sb("mgbc", [128, D], F32, "PB")
        t["xm"] = sb("xm", [128, 2, D], F32, "PB")
        t["lamv"] = sb("lamv", [128, 128], F32, "PB")
        t["lamt"] = sb("lamt", [128, 8], F32, "PB")
        S.arena("G", common_end)
        t["akT"] = sb("akT", [128, 2, 12 * 128], BF16, "G")
        t["av"] = sb("av", [128, 12, 4, 65], BF16, "G")
        t["bval"] = sb("bval", [128, 7, 4, 128], F32, "G")
        t["rowvalid"] = sb("rowvalid", [128, NTT * A_MAXSLOT * 2], F32, "G")
        t["cx"] = sb("cx", [128, 6, 256], BF16, "G")
        t["avh"] = sb("avh", [128, 2, 256], BF16, "G")
        t["ytT"] = sb("ytT", [128, 2, 512], BF16, "G")
        t["pwT"] = sb("pwT", [128, 7, 4, 128], BF16, "G")
        t["dwsT"] = sb("dwsT", [128, 4, 128], BF16, "G")
        t["cw"] = sb("cw", [64, 4, 64], BF16, "G")
        t["lnp"] = sb("lnp", [128, 512], F32, "G")
        t["qaT"] = sb("qaT", [128, 4, 512], BF16, "G")
        t["mqT"] = sb("mqT", [128, 4, 512], BF16, "G")
        t["zT"] = {k: sb(f"zT{k}", [128, 2, 512], BF16, "G") for k in "abm"}
        t["zTc"] = sb("zTc", [64, 4, 512], BF16, "G")
        t["yT"] = {k: t["zT"][k] for k in "abm"}
        t["yT"]["d"] = sb("yTd", [128, 2, 512], BF16, "G")
        t["yTc"] = t["zTc"]
        t["macc"] = sb("macc", [128, 4, 512], F32, "G")
        t["mbf"] = sb("mbf", [128, 8, 512], BF16, "G")
        t["sig"] = [sb(f"sig{i}", [128, 512], F32, "G") for i in range(1)]
        t["atmp"] = [sb(f"atmp{i}", [128, 512], F32, "G") for i in range(2)]
        t["apt"] = [sb(f"apt{i}", [128, 4, 128], BF16, "G") for i in range(2)]
        t["ytok"] = sb("ytok", [128, 4, 256], BF16, "G")
        t["ard"] = sb("ard", [128, 4], F32, "G")
        t["dl"] = sb("dl", [64, 512], BF16, "G")
        t["dg"] = sb("dg", [128, 768], F32, "G")
        t["atmp0"] = t["atmp"][0]
        t["atmp1"] = t["atmp"][1]
        t["dst"] = sb("dst", [128, 8], F32, "G")
        t["dvn"] = sb("dvn", [128, 256], BF16, "G")
        t["dy"] = sb("dy", [128, 256], F32, "G")
        t["ytok"] = t["ytok"]
        t["wbuf"] = [sb(f"wbuf{i}", [128, 2048], BF16, "G") for i in range(2)]
        assert S.sb_hi <= 229344, S.sb_hi
        self.sb_hi = S.sb_hi

        p = self.p = {}
        for nm in ("pjA", "pjB", "sA", "sB", "avA", "avB", "yp"):
            p[nm] = S.ps(nm, [128, 512], F32)
        p["tp"] = S.ps("tp", [128, 1024], BF16)

        self.load_consts()
        if mode != "F":
            self.load_x()
        for li in range(nl):
            self.layer(li)
        if mode in ("M", "F", "MG"):
            keys = self.store_x()
            S.finish(outputs=keys)
        elif mode == "MB":
            S.dma("sp", d["obn_o"], t["obn"][:].rearrange("p a b -> p (a b)"), r=["obn"], w=["obn_out"])
            S.finish(outputs=["obn_out"])
        else:
            S.finish(outputs=["kt_own_d", "v_own_d"])

    def load_consts(self):
        S, t, d = self.S, self.t, self.d
        S.dma("pool", t["ident"][:], d["ident"], w=["ident"])
        S.dma("pool", t["bones64"][:], d["bones64"], w=["bones64"])
        S.dma("pool", t["bones32"][:], d["bones32"], w=["bones32"])

    def load_x(self):
        S, t, d = self.S, self.t, self.d
        xv = d["x"].rearrange("(t p) c -> p t c", p=128)
        for q in range(4):
            S.dma("sp", t["x"][:, 4 * q:4 * q + 4, :], xv[:, 4 * q:4 * q + 4, :],
                  w=[f"x{tt}" for tt in range(4 * q, 4 * q + 4)])

    def store_x(self):
        S, t, d = self.S, self.t, self.d
        xv = d["xo"].rearrange("(t p) c -> p t c", p=128)
        keys = []
        for q in range(4):
            S.dma("sp", xv[:, 4 * q:4 * q + 4, :], t["x"][:, 4 * q:4 * q + 4, :],
                  r=[f"x{tt}" for tt in range(4 * q, 4 * q + 4)], w=[f"xo{q}"])
            keys.append(f"xo{q}")
        return keys

    def wstream_begin(self, items):
        self.ws_items = items
        self.ws_next = 0
        self.ws_hold = 0

    def wget(self, i):
        S, t = self.S, self.t
        ring = t["ring"]
        R = len(ring)
        lim = getattr(self, "ws_hold", i) + R - 1
        while self.ws_next < len(self.ws_items) and self.ws_next <= lim:
            k = self.ws_next
            src, view = self.ws_items[k]
            slot = ring[k % R]
            S.dma("pool", view(slot), src, w=[f"ring{k % R}"])
            self.ws_next += 1
        assert self.ws_next > i
        slot = ring[i % R]
        return self.ws_items[i][1](slot), f"ring{i % R}"

    def wgroup(self, i, k):
        self.ws_hold = i
        out = [self.wget(i + a) for a in range(k)]
        return out

    def layer(self, li):
        S, t, d, p = self.S, self.t, self.d, self.p
        self.li = li
        self.layer_tables(li)
        self.mem_kv(li)
        w_in = d["w_in"][li]

        def wsrc(c0, n):
            return w_in[:, c0:c0 + n].rearrange("(c p) n -> p c n", p=128)

        def v256(slot):
            return slot[:, 0:2048].rearrange("p (c n) -> p c n", c=8)

        pcols = [C_AK, C_BK, C_BQ, C_AV, C_BV, C_CX]
        items = []
        for G in range(4):
            for c0 in pcols:
                items.append((wsrc(c0, 256), v256))
        self.wstream_begin(items)
        wi = 0
        for G in range(4):
            self.emit_hT(G)
            (wa, ka), = self.wgroup(wi, 1); wi += 1
            self.p_ak(G, wa, ka)
            (wa, ka), = self.wgroup(wi, 1); wi += 1
            self.p_b64(G, wa, ka, which="k")
            (wa, ka), = self.wgroup(wi, 1); wi += 1
            self.p_b64(G, wa, ka, which="q")
            w3 = self.wgroup(wi, 3); wi += 3
            self.p_vtok(G, w3)
        if self.mode == "P":
            return
        self.exchange(li)
        import os
        dbg = os.environ.get("KDBG", "")
        if "nob" not in dbg and self.mode != "MG":
            self.b_phase(li)
        if self.mode == "MG" and os.environ.get("NOOBN") != "1":
            S.dma("sp", t["obn"][:].rearrange("p a b -> p (a b)"), d["obn_i"], w=["obn"])
        S.barrier()
        if "nog" not in dbg and self.mode != "MB":
            self.g_phase(li)
        S.barrier()

    def layer_tables(self, li):
        S, t, d = self.S, self.t, self.d
        S.dma("sp", t["colp"][:], d["colp"][li], w=["colp"])
        S.dma("sp", t["gbc"][:], d["rowp"][li][:, 0:1024].broadcast_to([128, 1024]), w=["gbc"])
        S.dma("sp", t["mgbc"][:], d["rowp"][li][:, 1024:2048].broadcast_to([128, 1024]), w=["mgbc"])
        S.dma("sp", t["lamv"][:], d["lamp"][li].broadcast_to([128, 128]), w=["lamv"])
        lamv, lamt = t["lamv"], t["lamt"]
        S.op("dve", lambda e: e.tensor_tensor(out=lamv[:, 0:32], in0=lamv[:, 0:32], in1=lamv[:, 32:64], op=ALU.mult),
             r=["lamv"], w=["lamv"])
        S.op("dve", lambda e: e.tensor_tensor(out=lamv[:, 64:96], in0=lamv[:, 64:96], in1=lamv[:, 96:128], op=ALU.mult),
             r=["lamv"], w=["lamv"])
        S.op("dve", lambda e: e.tensor_reduce(out=lamt[:, 0:1], in_=lamv[:, 0:32], axis=AX.X, op=ALU.add),
             r=["lamv"], w=["lamt"])
        S.op("dve", lambda e: e.tensor_reduce(out=lamt[:, 1:2], in_=lamv[:, 64:96], axis=AX.X, op=ALU.add),
             r=["lamv"], w=["lamt"])
        S.op("act", lambda e: e.activation(out=lamt[:, 2:4], in_=lamt[:, 0:2], func=AF.Exp), r=["lamt"], w=["lamt"])
        S.op("dve", lambda e: e.tensor_tensor(out=lamt[:, 4:5], in0=lamt[:, 3:4], in1=lamt[:, 2:3], op=ALU.subtract),
             r=["lamt"], w=["lamt"])
        S.op("dve", lambda e: e.tensor_tensor(out=t["neglam"][:], in0=lamt[:, 4:5], in1=t["colp"][:, 57:58],
                                              op=ALU.subtract), r=["lamt", "colp"], w=["neglam"])

    def rms_rows(self, src_ap_fn, src_keys, ntile, gbc_key, gbc, hb_cb):
        S, t = self.S, self.t
        ss4, hb = t["ss4"], t["hb"]
        junk = hb
        for j in range(ntile):
            S.op("act", lambda e, j=j: e.activation(out=junk[:], in_=src_ap_fn(j), func=AF.Square,
                                                    accum_out=ss4[:, j:j + 1]),
                 r=[src_keys[j]], w=["hb", "ss4"])
        S.op("act", lambda e: e.activation(out=ss4[:, 0:ntile], in_=ss4[:, 0:ntile], func=AF.Sqrt,
                                           bias=EPS, scale=1.0 / D), r=["ss4"], w=["ss4"])
        S.op("dve", lambda e: e.reciprocal(ss4[:, 0:ntile], ss4[:, 0:ntile]), r=["ss4"], w=["ss4"])
        for j in range(ntile):
            S.op("dve", lambda e, j=j: e.scalar_tensor_tensor(out=hb[:], in0=src_ap_fn(j), scalar=ss4[:, j:j + 1],
                                                              in1=gbc[:], op0=ALU.mult, op1=ALU.mult),
                 r=[src_keys[j], "ss4", gbc_key], w=["hb"])
            hb_cb(j)

    def transpose_hb(self, dst_ap, dst_key):
        S, t, p = self.S, self.t, self.p
        hb, tp, ident = t["hb"], p["tp"], t["ident"]

        def tr(e):
            for c in range(8):
                i = e.transpose(tp[:, c * 128:(c + 1) * 128], hb[:, c * 128:(c + 1) * 128], ident[:])
            return i
        S.op("pe", tr, r=["hb", "ident"], w=["tp"])
        S.op("act", lambda e: e.activation(out=dst_ap, in_=tp[:].rearrange("p (c n) -> p c n", c=8), func=AF.Copy),
             r=[], w=["tp", dst_key])

    def emit_hT(self, G):
        t = self.t
        x, hT = t["x"], t["hT"]
        self.rms_rows(lambda j: x[:, 4 * G + j, :], [f"x{4 * G + j}" for j in range(4)], 4, "gbc", t["gbc"],
                      lambda j: self.transpose_hb(hT[:, :, j * 128:(j + 1) * 128], "hT"))

    def proj_fm(self, bank, w_ap, wkey, c0, m, rhs_fn, rkey, n):
        S, p = self.S, self.p
        ps = p[bank]

        def mm(e):
            for c in range(8):
                i = e.matmul(ps[0:m, 0:n], w_ap[:, c, c0:c0 + m], rhs_fn(c), start=(c == 0), stop=(c == 7))
            return i
        S.op("pe", mm, r=[wkey, rkey], w=[bank])

    def headnorm(self, bank, sbank, m, n, bones, bones_key, sq_scale, sq_bias, gcol, out_ap, out_key):
        S, t, p = self.S, self.t, self.p
        ps, ps2, sq, std = p[bank], p[sbank], t["sq"], t["std"]
        S.op("act", lambda e: e.activation(out=sq[0:m, 0:n], in_=ps[0:m, 0:n], func=AF.Square), r=[], w=[bank, "sq"])
        S.op("pe", lambda e: e.matmul(ps2[0:m, 0:n], bones, sq[0:m, 0:n], start=True, stop=True),
             r=["sq", bones_key], w=[sbank])
        S.op("act", lambda e: e.activation(out=std[0:m, 0:n], in_=ps2[0:m, 0:n], func=AF.Sqrt, bias=sq_bias,
                                           scale=sq_scale), r=[], w=[sbank, "std"])
        S.op("dve", lambda e: e.reciprocal(std[0:m, 0:n], std[0:m, 0:n]), r=["std"], w=["std"])
        gcols = gcol if isinstance(gcol, list) else [gcol]
        outs = out_ap if isinstance(out_ap, list) else [out_ap]
        for gc_, oa_ in zip(gcols, outs):
            S.op("dve", lambda e, gc_=gc_, oa_=oa_: e.scalar_tensor_tensor(out=oa_, in0=ps[0:m, 0:n], scalar=gc_,
                                                                         in1=std[0:m, 0:n], op0=ALU.mult, op1=ALU.mult),
                 r=["std", "colp"], w=[bank, out_key])

    def p_ak(self, G, w_ap, wkey):
        S, t, d = self.S, self.t, self.d
        hT = t["hT"]
        for ch in range(2):
            bank = "pjA" if ch == 0 else "pjB"
            self.proj_fm(bank, w_ap, wkey, ch * 128, 128, lambda c: hT[:, c, :], "hT", 512)
            self.headnorm(bank, "sA", 128, 512, t["bones64"][:], "bones64", 1.0 / 64, EPS,
                          t["colp"][:, 41:42], t["ktmp"][:], "ktmp")
            S.dma("sp", d["kt_own"][ch * 128:(ch + 1) * 128, G * 512:(G + 1) * 512], t["ktmp"][:],
                  r=["ktmp"], w=["kt_own_d"])

    def p_b64(self, G, w_ap, wkey, which):
        S, t, d = self.S, self.t, self.d
        hT = t["hT"]
        for h in range(4):
            bank = "pjA" if h % 2 == 0 else "pjB"
            self.proj_fm(bank, w_ap, wkey, h * 64, 64, lambda c: hT[:, c, :], "hT", 512)
            if which == "k":
                self.headnorm(bank, "sA", 64, 512, t["bones32"][:], "bones32", 1.0 / 32, EPS,
                              t["colp"][0:64, 45:46], t["ktmp"][0:64, :], "ktmp")
                S.dma("sp", d["kt_own"][256 + h * 64:256 + (h + 1) * 64, G * 512:(G + 1) * 512],
                      t["ktmp"][0:64, :], r=["ktmp"], w=["kt_own_d"])
            else:
                self.headnorm(bank, "sA", 64, 512, t["bones32"][:], "bones32", 1.0, 32 * EPS,
                              t["colp"][0:64, 44:45], t["bqT"][:, h, G * 512:(G + 1) * 512], "bqT")

    def p_vtok(self, G, w3):
        S, t, d, p = self.S, self.t, self.d, self.p
        hT, vtok = t["hT"], t["vtok"]
        for j in range(4):
            tt = 4 * G + j
            for half, bank in ((0, "pjA"), (1, "pjB")):
                blocks = [0, 1] if half == 0 else [2]
                ps = p[bank]

                def mm(e, blocks=blocks, ps=ps, j=j):
                    for bi, b in enumerate(blocks):
                        wa = w3[b][0]
                        for c in range(8):
                            i = e.matmul(ps[:, bi * 256:(bi + 1) * 256], hT[:, c, j * 128:(j + 1) * 128],
                                         wa[:, c, :], start=(c == 0), stop=(c == 7))
                    return i
                S.op("pe", mm, r=["hT"] + [w3[b][1] for b in blocks], w=[bank])
                n = 256 * len(blocks)
                S.op("act", lambda e, ps=ps, n=n, half=half: e.activation(out=vtok[:, half * 512:half * 512 + n],
                                                                          in_=ps[:, 0:n], func=AF.Copy),
                     r=[], w=[bank, "vtok"])
            S.dma("sp", d["v_own"][tt * 128:(tt + 1) * 128, :], vtok[:], r=["vtok"], w=["v_own_d"])

    def mem_kv(self, li):
        S, t, d, p = self.S, self.t, self.d, self.p
        xm, hmT = t["xm"], t["hmT"]
        S.dma("sp", xm[:], d["mem"].rearrange("(t p) c -> p t c", p=128), w=["xm"])
        self.rms_rows(lambda j: xm[:, j, :], ["xm", "xm"], 2, "mgbc", t["mgbc"],
                      lambda j: self.transpose_hb(hmT[:, :, j * 128:(j + 1) * 128], "hmT"))
        wk = d["m_wkv"][li]
        ring = t["ring"]
        views = []
        for b in range(2):
            v = ring[b][:, 0:2048].rearrange("p (c n) -> p c n", c=8)
            S.dma("pool", v, wk[:, b * 256:(b + 1) * 256].rearrange("(c p) n -> p c n", p=128), w=[f"ring{b}"])
            views.append(v)
        for ch in range(2):
            bank = "pjA" if ch == 0 else "pjB"
            self.proj_fm(bank, views[0], "ring0", ch * 128, 128, lambda c: hmT[:, c, :], "hmT", 256)
            self.headnorm(bank, "sA", 128, 256, t["bones64"][:], "bones64", 1.0 / 64, EPS,
                          t["colp"][:, 43:44], t["mkT"][:, ch, :], "mkT")
        mv = t["mv"]
        S.op("pool", lambda e: e.memset(mv[:, :, :, 64:65], 1.0), r=[], w=["mv"])
        for j in range(2):
            ps = p["pjA"]

            def mm(e, j=j, ps=ps):
                for c in range(8):
                    i = e.matmul(ps[:, 0:256], hmT[:, c, j * 128:(j + 1) * 128], views[1][:, c, :],
                                 start=(c == 0), stop=(c == 7))
                return i
            S.op("pe", mm, r=["hmT", "ring1"], w=["pjA"])
            S.op("act", lambda e, j=j, ps=ps: e.activation(out=mv[:, j, :, 0:64],
                                                           in_=ps[:, 0:256].rearrange("p (h e) -> p h e", h=4),
                                                           func=AF.Copy), r=[], w=["pjA", "mv"])

    def exchange(self, li):
        if self.mode in ("M", "MB", "MG"):
            return
        raise NotImplementedError

    def b_phase(self, li):
        S, t, d, p = self.S, self.t, self.d, self.p
        for nm in ("U", "Rt", "scB", "biasB", "sc2B"):
            S.dma("sp", t[nm][:], d[nm], w=[nm])
        kt_all, v_all = d["kt_all"], d["v_all"]
        bqT, obn = t["bqT"], t["obn"]
        U, Rt, scB, biasB, sc2B = t["U"], t["Rt"], t["scB"], t["biasB"], t["sc2B"]
        for i in range(2):
            S.op("pool", lambda e, i=i: e.memset(t["bv"][i][:, :, 64:65], 1.0), r=[], w=[f"bv{i}"])
        sidx = 0
        aidx = 0
        tix = 0
        pix = 0
        for h in range(4):
            bk, bkk = t["bkT"][h % 2], f"bkT{h % 2}"
            bvv, bvk = t["bv"][h % 2], f"bv{h % 2}"
            for r in range(2):
                S.dma("sp", bk[:, r * NT:(r + 1) * NT], kt_all[r, 256 + h * 64:256 + (h + 1) * 64, :],
                      r=["kt_all"], w=[bkk])
                S.dma("sp", bvv[:, r * 16:(r + 1) * 16, 0:64],
                      v_all[r, :, 256 + h * 64:256 + (h + 1) * 64].rearrange("(t p) e -> p t e", p=128),
                      r=["v_all"], w=[bvk])
            for qb in range(4):
                cands = b_cands(qb)
                for m in range(2):
                    avb = "avA" if aidx % 2 == 0 else "avB"
                    aidx += 1
                    av = p[avb]
                    for kt in range(32):
                        sb_ = "sA" if sidx % 2 == 0 else "sB"
                        sidx += 1
                        sps = p[sb_]
                        S.op("pe", lambda e, sps=sps, bk=bk, kt=kt, m=m, qb=qb, h=h: e.matmul(
                            sps[:, :], bk[m * 32:(m + 1) * 32, kt * 128:(kt + 1) * 128],
                            bqT[m * 32:(m + 1) * 32, h, qb * 512:(qb + 1) * 512], start=True, stop=True),
                            r=[bkk, "bqT"], w=[sb_])
                        tmp, tk = t["btmp"][tix % 2], f"btmp{tix % 2}"
                        tix += 1
                        col = (qb * 32 + kt) * 4 + h
                        S.op("dve", lambda e, tmp=tmp, sps=sps, col=col: e.scalar_tensor_tensor(
                            out=tmp[:], in0=U[:], scalar=scB[:, col:col + 1], in1=sps[:, :], op0=ALU.mult,
                            op1=ALU.add), r=["U", "scB"], w=[sb_, tk])
                        if kt in cands:
                            ci = cands.index(kt)
                            w0 = 384 - 128 * (ci % 4)
                            c2 = (qb * 8 + ci) * 4 + h
                            S.op("dve", lambda e, tmp=tmp, w0=w0, c2=c2: e.scalar_tensor_tensor(
                                out=tmp[:], in0=Rt[:, w0:w0 + 512], scalar=sc2B[:, c2:c2 + 1], in1=tmp[:],
                                op0=ALU.mult, op1=ALU.add), r=["Rt", "sc2B"], w=[tk])
                        pt, pk = t["bpt"][pix % 3], f"bpt{pix % 3}"
                        pix += 1
                        S.op("act", lambda e, pt=pt, tmp=tmp, col=col: e.activation(
                            out=pt[:], in_=tmp[:], func=AF.Exp, bias=biasB[:, col:col + 1], scale=1.0),
                            r=[tk, "biasB"], w=[pk])

                        def avmm(e, pt=pt, av=av, bvv=bvv, kt=kt):
                            for j in range(4):
                                i = e.matmul(av[:, j * 65:(j + 1) * 65], pt[:, j * 128:(j + 1) * 128],
                                             bvv[:, kt, :], start=(kt == 0 and j == 0), stop=(kt == 31),
                                             skip_group_check=True)
                            return i
                        S.op("pe", avmm, r=[pk, bvk], w=[avb])
                    av3 = av[:, 0:260].rearrange("p (j e) -> p j e", j=4)
                    brd, bo = t["brd"], t["bo"]
                    S.op("dve", lambda e, av3=av3: e.reciprocal(brd[:].rearrange("p (j o) -> p j o", o=1),
                                                                av3[:, :, 64:65]), r=[], w=[avb, "brd"])
                    S.op("dve", lambda e, av3=av3, m=m: e.tensor_tensor(
                        out=bo[:, m, :, :], in0=av3[:, :, 0:64],
                        in1=brd[:].rearrange("p (j o) -> p j o", o=1).broadcast_to([128, 4, 64]), op=ALU.mult),
                        r=["brd"], w=[avb, "bo"])
                bo, bob, bsq, brd = t["bo"], t["bob"], t["bsq"], t["brd"]
                S.op("dve", lambda e: e.scalar_tensor_tensor(out=bob[:], in0=bo[:, 1, :, :], scalar=t["neglam"][:, 0:1],
                                                             in1=bo[:, 0, :, :], op0=ALU.mult, op1=ALU.add),
                     r=["bo", "neglam"], w=["bob"])
                S.op("act", lambda e: e.activation(out=bsq[:], in_=bob[:], func=AF.Square), r=["bob"], w=["bsq"])
                S.op("dve", lambda e: e.tensor_reduce(out=brd[:], in_=bsq[:], axis=AX.X, op=ALU.add),
                     r=["bsq"], w=["brd"])
                S.op("act", lambda e: e.activation(out=brd[:], in_=brd[:], func=AF.Sqrt, bias=t["colp"][:, 56:57],
                                                   scale=t["colp"][:, 55:56]), r=["brd", "colp"], w=["brd"])
                S.op("dve", lambda e: e.reciprocal(brd[:], brd[:]), r=["brd"], w=["brd"])
                S.op("dve", lambda e, qb=qb, h=h: e.tensor_tensor(
                    out=obn[:, 4 * qb:4 * qb + 4, h * 64:(h + 1) * 64], in0=bob[:],
                    in1=brd[:].rearrange("p (j o) -> p j o", o=1).broadcast_to([128, 4, 64]), op=ALU.mult),
                    r=["bob", "brd"], w=["obn"])

    def g_phase(self, li):
        S, t, d, p = self.S, self.t, self.d, self.p
        kt_all, v_all, kt_own, v_own = d["kt_all"], d["v_all"], d["kt_own"], d["v_own"]
        S.dma("sp", t["bval"][:].rearrange("p a b c -> p (a b c)"), d["bval"][li], w=["bval"])
        S.dma("sp", t["rowvalid"][:], d["rowvalid"], w=["rowvalid"])
        S.dma("pool", t["pwT"][:].rearrange("p a b c -> p (a b c)"), d["pwT"], w=["pwT"])
        S.dma("pool", t["dwsT"][:].rearrange("p a b -> p (a b)"), d["dwsT"][li], w=["dwsT"])
        S.dma("pool", t["cw"][:].rearrange("p a b -> p (a b)"), d["cw"][li], w=["cw"])
        S.dma("sp", t["lnp"][:], d["rowp"][li][:, 2048:2560].broadcast_to([128, 512]), w=["lnp"])
        akT, av, cx = t["akT"], t["av"], t["cx"]
        S.op("pool", lambda e: e.memset(av[:, :, :, 64:65], 1.0), r=[], w=[f"av{a}" for a in range(12)])

        w_in = d["w_in"][li]

        def wsrc(c0, n):
            return w_in[:, c0:c0 + n].rearrange("(c p) n -> p c n", p=128)

        def v256(slot):
            return slot[:, 0:2048].rearrange("p (c n) -> p c n", c=8)

        wb, wo = d["w_branch"][li], d["w_out"][li]
        self.wb_n = 0

        def load_wb(i, half):
            n = self.wb_n
            self.wb_n += 1
            buf, key = t["wbuf"][n % 2], f"wbuf{n % 2}"
            c0 = half * 512
            if i == 2:
                v = buf[0:64, 0:2048].rearrange("p (g n) -> p g n", g=4)
                S.dma("pool", v, wb[2][:, c0:c0 + 512].rearrange("(g p) n -> p g n", p=64), w=[key])
            else:
                v = buf[:, 0:1024].rearrange("p (k n) -> p k n", k=2)
                S.dma("pool", v, wb[i][:, c0:c0 + 512].rearrange("(k p) n -> p k n", p=128), w=[key])
            return v, key

        slabs = [C_AQ, C_MQ, C_AZ, C_BZ, C_MZ, C_CZ, C_DU, C_DV, C_DZ]
        items = []
        for G in range(4):
            for c0 in slabs:
                items.append((wsrc(c0, 256), v256))
            for half in range(2):
                for i in range(5):
                    for q in range(2):
                        items.append((wsrc(C_GATE + i * 1024 + half * 512 + q * 256, 256), v256))
            for q in range(4):
                items.append((wo[:, q * 256:(q + 1) * 256].rearrange("(c p) n -> p c n", p=128), v256))
        self.wstream_begin(items)
        self.wi = 0

        def nextw():
            (r,) = self.wgroup(self.wi, 1)
            self.wi += 1
            return r

        for G in range(4):
            lo, hi = max(0, 4 * G - 3), min(15, 4 * G + 6)
            nown = hi - lo + 1
            self.a_lo, self.a_nown = lo, nown
            for ch in range(2):
                S.dma("sp", akT[:, ch, 0:nown * 128], kt_own[ch * 128:(ch + 1) * 128, lo * 128:(hi + 1) * 128],
                      r=["kt_own_d"], w=["akT"])
            for a in range(nown):
                S.dma("sp", av[:, a, :, 0:64],
                      v_own[(lo + a) * 128:(lo + a + 1) * 128, 0:256].rearrange("p (h e) -> p h e", h=4),
                      r=["v_own_d"], w=[f"av{a}"])
            if G in (0, 3):
                r_, c0 = (0, 1792) if G == 0 else (1, 0)
                for ch in range(2):
                    S.dma("sp", akT[:, ch, nown * 128:(nown + 2) * 128], kt_all[r_, ch * 128:(ch + 1) * 128, c0:c0 + 256],
                          r=["kt_all"], w=["akT"])
                avh = t["avh"]
                for a in range(2):
                    S.dma("sp", avh[:, a, :], v_all[r_, c0 + a * 128:c0 + (a + 1) * 128, 0:256], r=["v_all"], w=["avh"])
                    S.op("pool", lambda e, a=a, nown=nown: e.tensor_copy(
                        out=av[:, nown + a, :, 0:64], in_=avh[:, a, :].rearrange("p (h e) -> p h e", h=4)),
                        r=["avh"], w=[f"av{nown + a}"])
            clo, chi = max(0, 4 * G - 1), min(15, 4 * G + 4)
            s0 = clo - (4 * G - 1)
            S.dma("sp", cx[:, s0:s0 + (chi - clo + 1), :],
                  v_own[clo * 128:(chi + 1) * 128, 512:768].rearrange("(t p) c -> p t c", p=128), r=["v_own_d"], w=["cx"])
            if G == 0:
                S.dma("sp", cx[:, 0, :], v_all[0, 1920:2048, 512:768], r=["v_all"], w=["cx"])
            if G == 3:
                S.dma("sp", cx[:, 5, :], v_all[1, 0:128, 512:768], r=["v_all"], w=["cx"])

            self.emit_hT(G)
            hT = t["hT"]
            for (dst, gc, gm) in ((t["qaT"], 40, 58), (t["mqT"], 42, 60)):
                wa, wk = nextw()
                for ch in range(2):
                    bank = "pjA" if ch == 0 else "pjB"
                    self.proj_fm(bank, wa, wk, ch * 128, 128, lambda c: hT[:, c, :], "hT", 512)
                    self.headnorm(bank, "sA", 128, 512, t["bones64"][:], "bones64", 1.0, 64 * EPS,
                                  [t["colp"][:, gm:gm + 1], t["colp"][:, gm + 1:gm + 2]],
                                  [dst[:, 2 * ch, :], dst[:, 2 * ch + 1, :]], "q" + str(gc))
            for k in "abm":
                wa, wk = nextw()
                for ch in range(2):
                    bank = "pjA" if ch == 0 else "pjB"
                    self.proj_fm(bank, wa, wk, ch * 128, 128, lambda c: hT[:, c, :], "hT", 512)
                    S.op("act", lambda e, bank=bank, k=k, ch=ch: e.activation(out=t["zT"][k][:, ch, :], in_=p[bank][:, :],
                                                                              func=AF.Silu), r=[], w=[bank, "zT" + k])
            wa, wk = nextw()
            for g in range(4):
                bank = "pjA" if g % 2 == 0 else "pjB"
                self.proj_fm(bank, wa, wk, g * 64, 64, lambda c: hT[:, c, :], "hT", 512)
                S.op("act", lambda e, bank=bank, g=g: e.activation(out=t["zTc"][:, g, :], in_=p[bank][0:64, :],
                                                                    func=AF.Silu), r=[], w=[bank, "zTc"])
            import os
            gs = int(os.environ.get("GSTOP", "99"))
            if gs <= 1:
                break
            w3 = self.wgroup(self.wi, 3)
            self.wi += 3
            self.d_branch(G, w3)
            if gs <= 2:
                break
            if os.environ.get("SKIPA") != "1":
                self.a_branch(G)
            if gs <= 3:
                break
            self.m_branch(G)
            if gs <= 4:
                break
            self.c_branch(G)
            if gs <= 5:
                break
            self.b_finish(G)
            if gs <= 6:
                break
            order = [("a", 0), ("b", 1), ("c", 2), ("d", 3), ("m", 4)]
            ykey = {"a": "zTa", "b": "zTb", "m": "zTm", "d": "yTd"}
            seq = [(half, nm, i) for half in range(2) for (nm, i) in order]
            wb_next = load_wb(0, 0)
            for si, (half, nm, i) in enumerate(seq):
                wb_cur = wb_next
                if si + 1 < len(seq):
                    wb_next = load_wb(seq[si + 1][2], seq[si + 1][0])
                for q in range(2):
                    wg, wgk = nextw()
                    for dd in range(2):
                        dl_ = 2 * q + dd
                        dc = half * 4 + dl_
                        bank = "pjA" if dd == 0 else "pjB"
                        self.proj_fm(bank, wg, wgk, dd * 128, 128, lambda c: hT[:, c, :], "hT", 512)
                        sg, sgk = t["sig"][0], "sig0"
                        S.op("act", lambda e, bank=bank, sg=sg, i=i, dc=dc: e.activation(
                            out=sg[:], in_=p[bank][:, :], func=AF.Sigmoid,
                            bias=t["colp"][:, i * 8 + dc:i * 8 + dc + 1], scale=1.0),
                            r=["colp"], w=[bank, sgk])
                        yp = p["yp"]
                        wv, wvk = wb_cur
                        cc = dl_ * 128
                        if i == 2:
                            def mm(e, wv=wv, cc=cc):
                                for g in range(4):
                                    ii = e.matmul(yp[:, :], wv[:, g, cc:cc + 128], t["yTc"][:, g, :],
                                                  start=(g == 0), stop=(g == 3))
                                return ii
                            S.op("pe", mm, r=[wvk, "zTc"], w=["yp"])
                        else:
                            yT = t["yT"][nm]

                            def mm(e, wv=wv, yT=yT, cc=cc):
                                for k in range(2):
                                    ii = e.matmul(yp[:, :], wv[:, k, cc:cc + 128], yT[:, k, :],
                                                  start=(k == 0), stop=(k == 1))
                                return ii
                            S.op("pe", mm, r=[wvk, ykey[nm]], w=["yp"])
                        macc, mbf = t["macc"], t["mbf"]
                        mk_ = f"macc{dl_}"
                        if i == 0:
                            S.op("dve", lambda e, sg=sg, dl_=dl_: e.tensor_tensor(out=macc[:, dl_, :], in0=sg[:], in1=yp[:, :],
                                                                                  op=ALU.mult), r=[sgk], w=["yp", mk_])
                        else:
                            S.op("dve", lambda e, sg=sg: e.tensor_tensor(out=sg[:], in0=sg[:], in1=yp[:, :], op=ALU.mult),
                                 r=[], w=["yp", sgk])
                            if i < 4:
                                S.op("dve", lambda e, dl_=dl_, sg=sg: e.tensor_tensor(out=macc[:, dl_, :], in0=macc[:, dl_, :],
                                                                                      in1=sg[:], op=ALU.add),
                                     r=[sgk], w=[mk_])
                            else:
                                S.op("dve", lambda e, dl_=dl_, dc=dc, sg=sg: e.tensor_tensor(
                                    out=mbf[:, dc, :], in0=macc[:, dl_, :], in1=sg[:], op=ALU.add),
                                    r=[sgk, mk_], w=[f"mbf{dc}"])
            x = t["x"]
            for q in range(4):
                wv, wvk = nextw()
                for j in range(4):
                    tt = 4 * G + j
                    bank = "pjA" if j % 2 == 0 else "pjB"
                    ps = p[bank]

                    def mm(e, ps=ps, wv=wv, j=j):
                        for c in range(8):
                            ii = e.matmul(ps[:, 0:256], t["mbf"][:, c, j * 128:(j + 1) * 128], wv[:, c, :],
                                          start=(c == 0), stop=(c == 7))
                        return ii
                    S.op("pe", mm, r=[wvk] + [f"mbf{c}" for c in range(8)], w=[bank])
                    S.op("dve", lambda e, ps=ps, tt=tt, q=q: e.tensor_tensor(
                        out=x[:, tt, q * 256:(q + 1) * 256], in0=x[:, tt, q * 256:(q + 1) * 256], in1=ps[:, 0:256],
                        op=ALU.add), r=[], w=[bank, f"x{tt}"])

    def tok2fm(self, src_fn, src_key, zT, zkey, gcol, dstT, dkey, ntile=4):
        S, t, p = self.S, self.t, self.p
        tp, ident = p["tp"], t["ident"]

        def tr(e):
            for j in range(ntile):
                for kc in range(2):
                    i = e.transpose(tp[:, (kc * 4 + j) * 128:(kc * 4 + j + 1) * 128],
                                    src_fn(j)[:, kc * 128:(kc + 1) * 128], ident[:])
            return i
        S.op("pe", tr, r=[src_key, "ident"], w=["tp"])
        tpv = tp[:].rearrange("p (k n) -> p k n", k=2)
        ytT = t["ytT"]
        if gcol is None:
            S.op("act", lambda e: e.activation(out=ytT[:], in_=tpv, func=AF.Copy), r=[], w=["tp", "ytT"])
        else:
            S.op("act", lambda e: e.activation(out=ytT[:], in_=tpv, func=AF.Copy, scale=gcol), r=["colp"], w=["tp", "ytT"])
        S.op("dve", lambda e: e.tensor_tensor(out=dstT[:], in0=ytT[:], in1=zT[:], op=ALU.mult),
             r=["ytT"], w=[dkey])

    def b_finish(self, G):
        t = self.t
        obn = t["obn"]
        self.tok2fm(lambda j: obn[:, 4 * G + j, :], "obn", t["zT"]["b"], "zTb", t["colp"][:, 50:51], t["yT"]["b"], "zTb")

    def a_branch(self, G):
        S, t, p = self.S, self.t, self.p
        akT, av, qaT, bval, rv = t["akT"], t["av"], t["qaT"], t["bval"], t["rowvalid"]
        ytok, ard = t["ytok"], t["ard"]
        cnt = 0
        for j in range(4):
            tt = 4 * G + j
            avb = "avA" if j % 2 == 0 else "avB"
            avp = p[avb]
            slots = a_slots(tt)
            for si, (gslot, di) in enumerate(slots):
                slot = gslot - self.a_lo if gslot < 16 else self.a_nown + (gslot - 16) % 2
                assert 0 <= slot < 12
                sb_ = "sA" if cnt % 2 == 0 else "sB"
                sps = p[sb_]

                def qk(e, sps=sps, slot=slot, j=j):
                    for h in range(4):
                        i = e.matmul(sps[:, h * 128:(h + 1) * 128], akT[:, h // 2, slot * 128:(slot + 1) * 128],
                                     qaT[:, h, j * 128:(j + 1) * 128], start=True, stop=True,
                                     skip_group_check=True)
                    return i
                S.op("pe", qk, r=["akT", "q40"], w=[sb_])
                import os
                ast = int(os.environ.get("ASTOP", "9"))
                if ast <= 1:
                    cnt += 1
                    continue
                tmp, tk = t["atmp"][cnt % 2], f"atmp{cnt % 2}"
                S.op("dve", lambda e, tmp=tmp, sps=sps, di=di: e.tensor_tensor(
                    out=tmp[:], in0=sps[:, :], in1=bval[:, di, :, :].rearrange("p h q -> p (h q)"), op=ALU.add),
                    r=["bval"], w=[sb_, tk])
                if ast <= 2:
                    cnt += 1
                    continue
                pt, pk = t["apt"][cnt % 2], f"apt{cnt % 2}"
                tmp3 = tmp[:].rearrange("p (h q) -> p h q", h=4)
                for qr in range(2):
                    col = (tt * A_MAXSLOT + si) * 2 + qr
                    S.op("act", lambda e, pt=pt, tmp3=tmp3, qr=qr, col=col: e.activation(
                        out=pt[:, :, qr * 64:(qr + 1) * 64], in_=tmp3[:, :, qr * 64:(qr + 1) * 64], func=AF.Exp,
                        bias=rv[:, col:col + 1], scale=1.0), r=[tk, "rowvalid"], w=[pk])

                if ast <= 3:
                    cnt += 1
                    continue

                def avmm(e, pt=pt, avp=avp, slot=slot, si=si, last=(si == len(slots) - 1)):
                    for h in range(4):
                        i = e.matmul(avp[:, h * 65:(h + 1) * 65], pt[:, h, :], av[:, slot, h, :],
                                     start=(si == 0 and h == 0), stop=last, skip_group_check=True)
                    return i
                _m = os.environ.get("AVNODEP", "0")
                _r = [pk, f"av{slot}"]
                if _m == "1" or (_m == "3" and gslot >= 16) or (_m == "4" and gslot < 16):
                    _r = [pk]
                S.op("pe", avmm, r=_r, w=[avb])
                cnt += 1
            if int(os.environ.get("ASTOP", "9")) <= 4:
                continue
            av3 = avp[:, 0:260].rearrange("p (h e) -> p h e", h=4)
            S.op("dve", lambda e, av3=av3: e.reciprocal(ard[:].rearrange("p (h o) -> p h o", o=1), av3[:, :, 64:65]),
                 r=[], w=[avb, "ard"])
            S.op("dve", lambda e, av3=av3, j=j: e.tensor_tensor(
                out=ytok[:, j, :].rearrange("p (h e) -> p h e", h=4), in0=av3[:, :, 0:64],
                in1=ard[:].rearrange("p (h o) -> p h o", o=1).broadcast_to([128, 4, 64]), op=ALU.mult),
                r=["ard"], w=[avb, "ytok"])
        import os
        if int(os.environ.get("ASTOP", "9")) >= 6:
            self.tok2fm(lambda j: ytok[:, j, :], "ytok", t["zT"]["a"], "zTa", None, t["yT"]["a"], "zTa")

    def m_branch(self, G):
        S, t, p = self.S, self.t, self.p
        mkT, mv, mqT, ytok, ard = t["mkT"], t["mv"], t["mqT"], t["ytok"], t["ard"]
        cnt = 0
        for h in range(4):
            pb = (h % 2) * 64
            avb = "avA" if h % 2 == 0 else "avB"
            avp = p[avb]
            for kt in range(2):
                sb_ = "sA" if cnt % 2 == 0 else "sB"
                sps = p[sb_]
                S.op("pe", lambda e, sps=sps, pb=pb, h=h, kt=kt: e.matmul(
                    sps[:, :], mkT[:, h // 2, kt * 128:(kt + 1) * 128], mqT[:, h, :],
                    start=True, stop=True), r=["mkT", "q42"], w=[sb_])
                ptb = t["apt"][cnt % 2][:].rearrange("p h q -> p (h q)")
                pkb = f"apt{cnt % 2}"
                S.op("act", lambda e, ptb=ptb, sps=sps: e.activation(out=ptb, in_=sps[:, :], func=AF.Exp),
                     r=[], w=[sb_, pkb])

                def avmm(e, ptb=ptb, avp=avp, kt=kt, h=h):
                    for j in range(4):
                        i = e.matmul(avp[:, j * 65:(j + 1) * 65], ptb[:, j * 128:(j + 1) * 128], mv[:, kt, h, :],
                                     start=(kt == 0 and j == 0), stop=(kt == 1), skip_group_check=True)
                    return i
                S.op("pe", avmm, r=[pkb, "mv"], w=[avb])
                cnt += 1
            av3 = avp[:, 0:260].rearrange("p (j e) -> p j e", j=4)
            S.op("dve", lambda e, av3=av3: e.reciprocal(ard[:].rearrange("p (h o) -> p h o", o=1), av3[:, :, 64:65]),
                 r=[], w=[avb, "ard"])
            S.op("dve", lambda e, av3=av3, h=h: e.tensor_tensor(
                out=ytok[:, :, h * 64:(h + 1) * 64], in0=av3[:, :, 0:64],
                in1=ard[:].rearrange("p (h o) -> p h o", o=1).broadcast_to([128, 4, 64]), op=ALU.mult),
                r=["ard"], w=[avb, "ytok"])
        self.tok2fm(lambda j: ytok[:, j, :], "ytok", t["zT"]["m"], "zTm", None, t["yT"]["m"], "zTm")

    def c_branch(self, G):
        S, t, p = self.S, self.t, self.p
        cx, pwT, cw, dl, yTc, zTc = t["cx"], t["pwT"], t["cw"], t["dl"], t["yTc"], t["zTc"]
        for g in range(4):
            bank = "sA" if g % 2 == 0 else "sB"
            ps = p[bank]

            def mm(e, g=g, ps=ps):
                first = True
                for j in range(4):
                    tt = 4 * G + j
                    var = [0, 1, 2]
                    if tt == 0:
                        var = [3, 4, 2]
                    if tt == 15:
                        var = [0, 5, 6]
                    for k in range(3):
                        i = e.matmul(ps[0:64, j * 128:(j + 1) * 128], cx[:, j + k, g * 64:(g + 1) * 64],
                                     pwT[:, var[k], g, :], start=first, stop=(k == 2), skip_group_check=True)
                        first = False
                return i
            S.op("pe", mm, r=["cx", "pwT"], w=[bank])
            S.op("act", lambda e, ps=ps: e.activation(out=dl[:], in_=ps[0:64, :], func=AF.Copy), r=[], w=[bank, "dl"])
            yp = p["yp"]
            S.op("pe", lambda e, g=g: e.matmul(yp[0:64, :], cw[:, g, :], dl[:], start=True, stop=True),
                 r=["cw", "dl"], w=["yp"])
            S.op("dve", lambda e, g=g: e.scalar_tensor_tensor(out=yTc[:, g, :], in0=yp[0:64, :],
                                                              scalar=t["colp"][0:64, 46 + g:47 + g], in1=zTc[:, g, :],
                                                              op0=ALU.mult, op1=ALU.mult),
                 r=["colp"], w=["yp", "zTc"])

    def d_branch(self, G, w3):
        S, t, p = self.S, self.t, self.p
        hT, dg, dt1, dt2, dst, dvn, dy, lnp, dwsT = (t["hT"], t["dg"], t["atmp0"], t["atmp1"], t["dst"], t["dvn"],
                                                     t["dy"], t["lnp"], t["dwsT"])
        ytok = t["ytok"]
        for j in range(4):
            for half, bank in ((0, "pjA"), (1, "pjB")):
                blocks = [0, 1] if half == 0 else [2]
                ps = p[bank]

                def mm(e, blocks=blocks, ps=ps, j=j):
                    for bi, b in enumerate(blocks):
                        wa = w3[b][0]
                        for c in range(8):
                            i = e.matmul(ps[:, bi * 256:(bi + 1) * 256], hT[:, c, j * 128:(j + 1) * 128],
                                         wa[:, c, :], start=(c == 0), stop=(c == 7))
                    return i
                S.op("pe", mm, r=["hT"] + [w3[b][1] for b in blocks], w=[bank])
            pu, pz = p["pjA"], p["pjB"]
            S.op("act", lambda e: e.activation(out=dt1[:], in_=pu[:, :], func=AF.Square), r=[], w=["pjA", "atmp0"])
            S.op("dve", lambda e: e.tensor_scalar(out=dt1[:], in0=dt1[:], scalar1=0.044715, scalar2=1.0, op0=ALU.mult,
                                                  op1=ALU.add), r=["atmp0"], w=["atmp0"])
            S.op("dve", lambda e: e.tensor_tensor(out=dt1[:], in0=dt1[:], in1=pu[:, :], op=ALU.mult), r=["atmp0"],
                 w=["pjA", "atmp0"])
            S.op("act", lambda e: e.activation(out=dt2[:], in_=dt1[:], func=AF.Sigmoid, scale=1.5957691216057308),
                 r=["atmp0"], w=["atmp1"])
            S.op("dve", lambda e: e.tensor_tensor(out=dg[:, 0:512], in0=dt2[:], in1=pu[:, :], op=ALU.mult), r=["atmp1"],
                 w=["pjA", "dg"])
            S.op("act", lambda e: e.activation(out=dg[:, 512:768], in_=pz[:, 0:256], func=AF.Silu), r=[], w=["pjB", "dg"])
            S.op("dve", lambda e: e.bn_stats(out=dst[:, 0:6], in_=dg[:, 256:512]), r=["dg"], w=["dst"])
            S.op("dve", lambda e: e.bn_aggr(out=dst[:, 6:8], in_=dst[:, 0:6]), r=["dst"], w=["dst"])
            S.op("act", lambda e: e.activation(out=dst[:, 7:8], in_=dst[:, 7:8], func=AF.Sqrt, bias=EPS, scale=1.0),
                 r=["dst"], w=["dst"])
            S.op("dve", lambda e: e.reciprocal(dst[:, 7:8], dst[:, 7:8]), r=["dst"], w=["dst"])
            S.op("dve", lambda e: e.tensor_scalar(out=dt1[:, 0:256], in0=dg[:, 256:512], scalar1=dst[:, 6:7],
                                                  scalar2=dst[:, 7:8], op0=ALU.subtract, op1=ALU.mult),
                 r=["dg", "dst"], w=["atmp0"])
            S.op("dve", lambda e: e.tensor_tensor(out=dt1[:, 0:256], in0=dt1[:, 0:256], in1=lnp[:, 0:256], op=ALU.mult),
                 r=["lnp", "atmp0"], w=["atmp0"])
            S.op("dve", lambda e: e.tensor_tensor(out=dvn[:], in0=dt1[:, 0:256], in1=lnp[:, 256:512], op=ALU.add),
                 r=["lnp", "atmp0"], w=["dvn"])
            yp = p["yp"]

            def mix(e):
                for g in range(4):
                    i = e.matmul(yp[:, g * 64:(g + 1) * 64], dwsT[:, g, :], dvn[:, g * 64:(g + 1) * 64], start=True,
                                 stop=True, skip_group_check=True)
                return i
            S.op("pe", mix, r=["dwsT", "dvn"], w=["yp"])
            for g in range(4):
                S.op("dve", lambda e, g=g: e.scalar_tensor_tensor(
                    out=dy[:, g * 64:(g + 1) * 64], in0=yp[:, g * 64:(g + 1) * 64], scalar=t["colp"][:, 51 + g:52 + g],
                    in1=dg[:, g * 64:(g + 1) * 64], op0=ALU.add, op1=ALU.mult), r=["colp", "dg"], w=["yp", "dy"])
            S.op("dve", lambda e, j=j: e.tensor_tensor(out=t["ytok"][:, j, :], in0=dy[:], in1=dg[:, 512:768],
                                                       op=ALU.mult), r=["dy", "dg"], w=["ytok"])
        S2, tp, ident = self.S, p["tp"], t["ident"]
        ytd, yTd = t["ytok"], t["yT"]["d"]

        def tr(e):
            for j in range(4):
                for kc in range(2):
                    i = e.transpose(tp[:, (kc * 4 + j) * 128:(kc * 4 + j + 1) * 128], ytd[:, j, kc * 128:(kc + 1) * 128],
                                    ident[:])
            return i
        S.op("pe", tr, r=["ytok", "ident"], w=["tp"])
        S.op("act", lambda e: e.activation(out=yTd[:], in_=tp[:].rearrange("p (k n) -> p k n", k=2), func=AF.Copy),
             r=[], w=["tp", "yTd"])

    def dump_y(self, G):
        S, t, d = self.S, self.t, self.d
        S.op("dve", lambda e: e.nop(), r=[], w=[])


_PROGS = {}


def get_prog(mode, nl, debug=False):
    key = (mode, nl, debug)
    if key not in _PROGS:
        _PROGS[key] = Prog(mode, nl, debug)
    return _PROGS[key]


def core_inputs(inp, layers, xs, consts, ltabs, extra=None):
    maps = []
    for core in range(8):
        b, hf = core // 2, core % 2
        m = {"x": xs[core], "mem": np.ascontiguousarray(inp["mem"][b])}
        m["w_in"] = np.ascontiguousarray(inp["w_in"][layers])
        m["m_wkv"] = np.ascontiguousarray(inp["m_wkv"][layers])
        m["w_branch"] = np.ascontiguousarray(inp["w_branch"][layers])
        m["w_out"] = np.ascontiguousarray(inp["w_out"][layers])
        for k in ("rowp", "colp", "lamp", "bval", "dwsT", "cw"):
            m[k] = np.stack([ltabs[l][k] for l in layers])
        m["cpack"] = consts[hf]
        if extra is not None:
            m.update(extra[core])
        maps.append(m)
    return maps


def kernel(**inputs):
    inp = {k: np.asarray(v) for k, v in inputs.items()}
    x = inp["x"].astype(np.float32)
    consts = [pack_consts(host_consts(0)), pack_consts(host_consts(1))]
    ltabs = [host_layer_tables(inp, l) for l in range(2)]
    xs = [np.ascontiguousarray(x[c // 2, (c % 2) * NT:(c % 2 + 1) * NT]) for c in range(8)]
    for l in range(2):
        pp = get_prog("P", 1)
        res = run_bass_kernel_spmd(pp.nc, core_inputs(inp, [l], xs, consts, ltabs), core_ids=list(range(8)))
        extra = []
        for c in range(8):
            b = c // 2
            kt = np.stack([res.results[2 * b]["kt_own"], res.results[2 * b + 1]["kt_own"]])
            v = np.stack([res.results[2 * b]["v_own"], res.results[2 * b + 1]["v_own"]])
            extra.append({"kt_all": kt, "v_all": v})
        mb = get_prog("MB", 1)
        res = run_bass_kernel_spmd(mb.nc, core_inputs(inp, [l], xs, consts, ltabs, extra), core_ids=list(range(8)))
        for c in range(8):
            extra[c]["obn_i"] = res.results[c]["obn_o"]
        mg = get_prog("MG", 1)
        res = run_bass_kernel_spmd(mg.nc, core_inputs(inp, [l], xs, consts, ltabs, extra), core_ids=list(range(8)))
        xs = [np.asarray(res.results[c]["xo"], dtype=np.float32) for c in range(8)]
    out = np.zeros((4, 4096, D), np.float32)
    for c in range(8):
        out[c // 2, (c % 2) * NT:(c % 2 + 1) * NT] = xs[c]
    return out
```

```python
import math
from contextlib import ExitStack

import numpy as np
import ml_dtypes

import concourse.bass as bass
import concourse.mybir as mybir
from concourse.bass_utils import run_bass_kernel_spmd

F32 = mybir.dt.float32
BF16 = mybir.dt.bfloat16
AF = mybir.ActivationFunctionType
ALU = mybir.AluOpType
AX = mybir.AxisListType

EPS = 1e-6
NEG = -30000.0
NT = 2048
NTT = 16
D = 1024
SLOPES = [2.0 ** (-8.0 * (h + 1) / 4) for h in range(4)]

C_AQ, C_AK, C_AV, C_AZ = 0, 256, 512, 768
C_BQ, C_BK, C_BV, C_BZ = 1024, 1280, 1536, 1792
C_CX, C_CZ = 2048, 2304
C_DU, C_DV, C_DZ = 2560, 2816, 3072
C_MQ, C_MZ = 3328, 3584
C_GATE = 3840

COMPUTE = ("pe", "act", "dve", "pool")
QUEUES = ("pe", "act", "dve", "pool", "sp")
EPOCH = 16000
N_DMA_SEMS = 24


class _Op:
    __slots__ = ("eng", "fn", "deps", "is_dma", "signal", "tok", "idx", "ndma", "dsem")

    def __init__(self, eng, fn, is_dma):
        self.eng = eng
        self.fn = fn
        self.deps = []
        self.is_dma = is_dma
        self.signal = False
        self.tok = None
        self.ndma = 0
        self.dsem = None


class Sched:
    def __init__(self, nc, es, same_engine_sync=True):
        self.nc = nc
        self.es = es
        self.ops = []
        self.last_w = {}
        self.readers = {}
        self.same_engine_sync = same_engine_sync
        import os as _os
        nsp = int(_os.environ.get("NSP", "3"))
        npl = int(_os.environ.get("NPL", "3"))
        self.dma_pools = {"sp": list(range(0, nsp)), "pool": list(range(14, 14 + npl)), "act": list(range(22, 24))}
        self.dma_rr = {q: 0 for q in self.dma_pools}
        self.dma_last = [None] * N_DMA_SEMS
        self.sb_off = {}
        self.sb_hi = 0

    def arena(self, name, base):
        self.sb_off[name] = base

    def sb(self, name, shape, dt, arena="main"):
        nbytes = int(np.prod(shape[1:])) * (2 if dt == BF16 else 4)
        nbytes = (nbytes + 63) // 64 * 64
        off = self.sb_off[arena]
        self.sb_off[arena] = off + nbytes
        self.sb_hi = max(self.sb_hi, off + nbytes)
        return self.nc.alloc_sbuf_tensor_at(name, list(shape), dt, offset=off)

    def ps(self, name, shape, dt):
        return self.es.enter_context(self.nc.psum_tensor(name, list(shape), dt))

    def _add(self, op, r, w):
        deps = set()
        for k in r:
            lw = self.last_w.get(k)
            if lw is not None:
                deps.add(lw)
        for k in w:
            lw = self.last_w.get(k)
            if lw is not None:
                deps.add(lw)
            for rd in self.readers.get(k, ()):
                deps.add(rd)
        deps.discard(op)
        op.deps = sorted(deps, key=lambda o: o.idx)
        for k in r:
            self.readers.setdefault(k, []).append(op)
        for k in w:
            self.last_w[k] = op
            self.readers[k] = []
        return op

    def op(self, eng, fn, r=(), w=()):
        o = _Op(eng, fn, False)
        o.idx = len(self.ops)
        self.ops.append(o)
        return self._add(o, r, w)

    def dma(self, queue, out, in_, r=(), w=(), **kw):
        def fn(e, out=out, in_=in_, kw=kw):
            return [e.dma_start(out=out, in_=in_, **kw)]
        return self.dma_multi(queue, fn, 1, r, w)

    def dma_multi(self, queue, fn, n, r=(), w=()):
        o = _Op(queue, fn, True)
        o.idx = len(self.ops)
        o.ndma = n
        self.ops.append(o)
        pool_ = self.dma_pools[queue]
        s = pool_[self.dma_rr[queue] % len(pool_)]
        self.dma_rr[queue] += 1
        o.dsem = s
        self._add(o, r, w)
        prev = self.dma_last[s]
        if prev is not None and prev not in o.deps:
            o.deps.append(prev)
        self.dma_last[s] = o
        return o

    def barrier(self):
        lasts = {}
        for o in self.ops:
            if o.is_dma:
                lasts[("d", o.dsem)] = o
            elif o.fn is not None and o.eng in COMPUTE:
                lasts[("c", o.eng)] = o
        deps = sorted(lasts.values(), key=lambda o: o.idx)
        for q in QUEUES:
            o = _Op(q, (lambda e: e.nop()), False)
            o.idx = len(self.ops)
            o.deps = [dd for dd in deps if dd.is_dma or dd.eng != q]
            self.ops.append(o)

    def finish(self, outputs=()):
        nc = self.nc
        fin = _Op("sp", None, False)
        fin.idx = len(self.ops)
        fin.deps = [self.last_w[k] for k in outputs]
        self.ops.append(fin)
        for o in self.ops:
            for d in o.deps:
                if not d.is_dma:
                    if d.eng == o.eng and (d.eng == "pe" or not self.same_engine_sync):
                        continue
                    d.signal = True
        counts = {e: 0 for e in COMPUTE}
        for o in self.ops:
            if o.is_dma or o.fn is None:
                continue
            if o.signal:
                counts[o.eng] += 1
                o.tok = ("c", o.eng, counts[o.eng])
        dcount = [0] * N_DMA_SEMS
        for o in self.ops:
            if o.is_dma:
                dcount[o.dsem] += 16 * o.ndma
                o.tok = ("d", o.dsem, dcount[o.dsem])
        es = self.es
        csem = {}
        for e in COMPUTE:
            n_ep = counts[e] // EPOCH + 1
            csem[e] = [es.enter_context(nc.semaphore(f"s_{e}_{i}")) for i in range(n_ep)]
        dsem = [es.enter_context(nc.semaphore(f"s_dma_{i}")) for i in range(N_DMA_SEMS)]

        def resolve(tok):
            if tok[0] == "c":
                ep = (tok[2] - 1) // EPOCH
                return csem[tok[1]][ep], tok[2] - ep * EPOCH, ("c", tok[1], ep)
            return dsem[tok[1]], tok[2], ("d", tok[1])

        streams = {q: [o for o in self.ops if o.eng == q] for q in QUEUES}
        stats = {q: [len(streams[q]), 0] for q in QUEUES}

        def emit(q, e):
            waited = {}
            for o in streams[q]:
                for d in o.deps:
                    if d.tok is None:
                        continue
                    if (not d.is_dma) and d.eng == q and (q == "pe" or not self.same_engine_sync):
                        continue
                    sem, val, key = resolve(d.tok)
                    if waited.get(key, 0) >= val:
                        continue
                    if key[0] == "c":
                        skip = False
                        for k2 in waited:
                            if k2[0] == "c" and k2[1] == key[1] and k2[2] > key[2]:
                                skip = True
                        if skip:
                            continue
                    e.wait_ge(sem, val)
                    stats[q][1] += 1
                    waited[key] = val
                if o.fn is None:
                    continue
                if o.is_dma:
                    insts = o.fn(e)
                    assert len(insts) == o.ndma, (len(insts), o.ndma)
                    for i in insts:
                        i.then_inc(dsem[o.dsem], 16)
                else:
                    inst = o.fn(e)
                    if o.signal:
                        sem, _, _ = resolve(o.tok)
                        inst.then_inc(sem, 1)

        with nc.Block() as block:
            @block.sync
            def _(e):
                emit("sp", e)

            @block.tensor
            def _(e):
                emit("pe", e)

            @block.scalar
            def _(e):
                emit("act", e)

            @block.vector
            def _(e):
                emit("dve", e)

            @block.gpsimd
            def _(e):
                emit("pool", e)
        self.stats = stats
        return stats


A_D_LIST = [-3, -2, -1, 0, 1, 2, 3]


def a_slots(t):
    if t == 0:
        own = [0, 1, 2, 3]
        halo = [(16, -2), (17, -1)]
    elif t == 1:
        own = [0, 1, 2, 3]
        halo = [(17, -2)]
    elif t == 14:
        own = [12, 13, 14, 15]
        halo = [(18, 2)]
    elif t == 15:
        own = [12, 13, 14, 15]
        halo = [(18, 1), (19, 2)]
    else:
        own = list(range(t - 2, t + 3))
        halo = []
    out = [(s, s - t) for s in own] + halo
    return [(s, A_D_LIST.index(d)) for s, d in out]


A_MAXSLOT = 6


def host_consts(hf):
    c = {}
    c["ident"] = np.eye(128, dtype=np.float32)
    b64 = np.zeros((128, 128), np.float32)
    b64[:64, :64] = 1
    b64[64:, 64:] = 1
    c["bones64"] = b64
    b32 = np.zeros((64, 64), np.float32)
    b32[:32, :32] = 1
    b32[32:, 32:] = 1
    c["bones32"] = b32
    i = np.arange(128)[:, None]
    j = np.arange(512)[None, :]
    c["U"] = (j - i).astype(np.float32)
    m = np.arange(896)[None, :]
    c["Rt"] = np.maximum(i - m + 384, 0).astype(np.float32)
    scB = np.zeros((4, 32, 4), np.float32)
    biasB = np.zeros((4, 32, 4), np.float32)
    sc2B = np.zeros((4, 8, 4), np.float32)
    for qb in range(4):
        Q0 = hf * 2048 + qb * 512
        for kt in range(32):
            K0 = kt * 128
            Dd = Q0 - K0
            for h in range(4):
                sl = SLOPES[h]
                if K0 + 127 < Q0 or (Q0 <= K0 < Q0 + 512):
                    scB[qb, kt, h] = -sl
                    biasB[qb, kt, h] = -sl * Dd
                else:
                    scB[qb, kt, h] = sl
                    biasB[qb, kt, h] = sl * Dd
        for ci, kt in enumerate(b_cands(qb)):
            K0 = kt * 128
            if Q0 <= K0 < Q0 + 512:
                for h in range(4):
                    sc2B[qb, ci, h] = -2.0 * SLOPES[h]
    c["scB"] = np.broadcast_to(scB.reshape(1, 512), (128, 512)).copy()
    c["biasB"] = np.broadcast_to(biasB.reshape(1, 512), (128, 512)).copy()
    c["sc2B"] = np.broadcast_to(sc2B.reshape(1, 128), (128, 128)).copy()
    rv = np.zeros((128, NTT, A_MAXSLOT, 2), np.float32)
    for t in range(NTT):
        R0 = 32 * hf + 2 * t
        for si, (slot, _) in enumerate(a_slots(t)):
            if slot < 16:
                pair = 16 * hf + slot
            else:
                pair = 14 + (slot - 16)
            own_dup = (slot >= 16) and (pair // 16 == hf)
            for qr in range(2):
                r = R0 + qr
                rs = min(max(r - 4, 0), 56)
                for kr in range(2):
                    rp = 2 * pair + kr
                    ok = (rs <= rp < rs + 8) and not own_dup
                    rv[kr * 64:(kr + 1) * 64, t, si, qr] = 0.0 if ok else NEG
    c["rowvalid"] = rv.reshape(128, NTT * A_MAXSLOT * 2)
    T = 4096
    pw = np.zeros((128, 7, 4, 128), np.float32)

    def pwmat(src0, dst0, w):
        M = np.zeros((128, 128), np.float64)
        for tt in range(128):
            t = dst0 + tt
            lo = min(max(t - w // 2, 0), T - 1)
            hi = min(max(t - w // 2 + w - 1, 0), T - 1)
            cnt = hi - lo + 1
            for s in range(lo, hi + 1):
                if src0 <= s < src0 + 128:
                    M[s - src0, tt] += 1.0 / cnt
            if src0 <= t < src0 + 128:
                M[t - src0, tt] -= 1.0
        return M.astype(np.float32)

    base = hf * 2048
    for g, w in enumerate((2, 4, 8, 16)):
        mid = 1024
        pw[:, 0, g] = pwmat(mid - 128, mid, w)
        pw[:, 1, g] = pwmat(mid, mid, w)
        pw[:, 2, g] = pwmat(mid + 128, mid, w)
        pw[:, 3, g] = pwmat(base - 128, base, w) if hf == 1 else 0.0
        pw[:, 4, g] = pwmat(base, base, w)
        pw[:, 5, g] = pwmat(base + 1920, base + 1920, w)
        pw[:, 6, g] = pwmat(base + 2048, base + 1920, w) if hf == 0 else 0.0
    c["pwT"] = pw.reshape(128, 7 * 4 * 128)
    return c


CPACK = [("ident", 128), ("bones64", 128), ("bones32", 64), ("U", 512), ("Rt", 896), ("scB", 512),
         ("biasB", 512), ("sc2B", 128), ("rowvalid", NTT * A_MAXSLOT * 2), ("pwT", 3584)]
CPACK_W = sum(w_ for _, w_ in CPACK)


def pack_consts(c):
    out = np.zeros((128, CPACK_W), np.float32)
    off = 0
    for nm, w_ in CPACK:
        a = c[nm]
        out[:a.shape[0], off:off + w_] = a
        off += w_
    return out


def b_cands(qb):
    return [4 * qb + k for k in range(4)] + [16 + 4 * qb + k for k in range(4)]


def host_layer_tables(inp, l):
    t = {}
    t["rowp"] = np.concatenate([inp["norm_g"][l], inp["m_norm_g"][l], inp["d_ln_g"][l],
                                inp["d_ln_b"][l]]).astype(np.float32)[None, :]
    colp = np.zeros((128, 64), np.float32)
    colp[:, 0:40] = inp["b_gate"][l].reshape(5, 8, 128).transpose(2, 0, 1).reshape(128, 40)
    p = np.arange(128)
    colp[:, 40] = inp["a_qn_g"][l][p % 64]
    colp[:, 41] = inp["a_kn_g"][l][p % 64]
    colp[:, 42] = inp["m_qn_g"][l][p % 64]
    colp[:, 43] = inp["m_kn_g"][l][p % 64]
    colp[:, 44] = inp["b_qn_g"][l][p % 32]
    colp[:, 45] = inp["b_kn_g"][l][p % 32]
    for g in range(4):
        colp[:, 46 + g] = inp["c_scale"][l][g * 64 + p % 64]
        colp[:, 51 + g] = inp["d_bs"][l][g]
    colp[:, 50] = inp["b_sub_g"][l][p % 64]
    lo_ = (p < 64)
    colp[:, 58] = np.where(lo_, inp["a_qn_g"][l][p % 64], 0.0)
    colp[:, 59] = np.where(~lo_, inp["a_qn_g"][l][p % 64], 0.0)
    colp[:, 60] = np.where(lo_, inp["m_qn_g"][l][p % 64], 0.0)
    colp[:, 61] = np.where(~lo_, inp["m_qn_g"][l][p % 64], 0.0)
    li = 0.8 - 0.6 * math.exp(-0.3 * l)
    colp[:, 55] = 1.0 / (64.0 * (1.0 - li) ** 2)
    colp[:, 56] = EPS / (1.0 - li) ** 2
    colp[:, 57] = li
    t["colp"] = colp
    t["lamp"] = np.concatenate([inp["b_lam_q1"][l], inp["b_lam_k1"][l], inp["b_lam_q2"][l],
                                inp["b_lam_k2"][l]]).astype(np.float32)[None, :]
    rpb = inp["a_rpb"][l]
    cols = np.arange(64)
    cs = np.clip(cols - 8, 0, 48)
    colok = (cols[:, None] >= cs[None, :]) & (cols[:, None] < cs[None, :] + 16)
    coff = np.clip(cols[:, None] - cols[None, :], -15, 15) + 15
    bval = np.full((128, 7, 4, 128), NEG, np.float32)
    for di, d in enumerate(A_D_LIST):
        for kr in range(2):
            for qr in range(2):
                dr = 2 * d + kr - qr
                if abs(dr) > 7:
                    continue
                for h in range(4):
                    blk = np.where(colok, rpb[h, dr + 7][coff], np.float32(NEG))
                    bval[kr * 64:(kr + 1) * 64, di, h, qr * 64:(qr + 1) * 64] = blk
    t["bval"] = bval.reshape(128, 7 * 4 * 128)
    t["dwsT"] = np.ascontiguousarray(inp["d_ws"][l].transpose(2, 0, 1)).reshape(128, 512)
    t["cw"] = np.ascontiguousarray(inp["c_w"][l].transpose(1, 0, 2)).reshape(64, 256)
    return t


class Prog:
    def __init__(self, mode, nl, debug=False):
        self.mode = mode
        self.nl = nl
        self.debug = debug
        self.nc = bass.Bass("TRN2", target_bir_lowering=False)
        self.es = ExitStack()
        self.S = Sched(self.nc, self.es)
        self.build()

    def din(self, name, shape, dt=F32):
        return self.nc.dram_tensor(name, list(shape), dt, kind="ExternalInput").ap()

    def dout(self, name, shape, dt=F32):
        return self.nc.dram_tensor(name, list(shape), dt, kind="ExternalOutput").ap()

    def dint(self, name, shape, dt=F32):
        return self.nc.dram_tensor(name, list(shape), dt, kind="Internal").ap()

    def build(self):
        nc, S, nl, mode = self.nc, self.S, self.nl, self.mode
        d = self.d = {}
        d["x"] = self.din("x", [NT, D])
        d["mem"] = self.din("mem", [256, D])
        d["w_in"] = self.din("w_in", [nl, D, 8960])
        d["m_wkv"] = self.din("m_wkv", [nl, D, 512])
        d["w_branch"] = self.din("w_branch", [nl, 5, 256, D])
        d["w_out"] = self.din("w_out", [nl, D, D])
        d["rowp"] = self.din("rowp", [nl, 1, 2560])
        d["colp"] = self.din("colp", [nl, 128, 64])
        d["lamp"] = self.din("lamp", [nl, 1, 128])
        d["bval"] = self.din("bval", [nl, 128, 3584])
        d["dwsT"] = self.din("dwsT", [nl, 128, 512])
        d["cw"] = self.din("cw", [nl, 64, 256])
        cpack = self.din("cpack", [128, CPACK_W])
        off = 0
        for nm, w_ in CPACK:
            d[nm] = cpack[0:64, off:off + w_] if nm == "bones32" else cpack[:, off:off + w_]
            off += w_
        if mode == "P":
            d["kt_own"] = self.dout("kt_own", [512, NT], BF16)
            d["v_own"] = self.dout("v_own", [NT, 768], BF16)
        elif mode in ("M", "MB", "MG"):
            d["kt_own"] = self.dint("kt_own", [512, NT], BF16)
            d["v_own"] = self.dint("v_own", [NT, 768], BF16)
            d["kt_all"] = self.din("kt_all", [2, 512, NT], BF16)
            d["v_all"] = self.din("v_all", [2, NT, 768], BF16)
            if mode == "MG":
                d["obn_i"] = self.din("obn_i", [128, NTT * 256], BF16)
            if mode == "MB":
                d["obn_o"] = self.dout("obn_o", [128, NTT * 256], BF16)
            else:
                d["xo"] = self.dout("xo", [NT, D])
        else:
            self.dl = {"kt_own": [], "v_own": [], "kt_all": [], "v_all": []}
            for l_ in range(nl):
                self.dl["kt_own"].append(self.dint(f"kt_own{l_}", [512, NT], BF16))
                self.dl["v_own"].append(self.dint(f"v_own{l_}", [NT, 768], BF16))
                self.dl["kt_all"].append(self.dint(f"kt_all{l_}", [2, 512, NT], BF16))
                self.dl["v_all"].append(self.dint(f"v_all{l_}", [2, NT, 768], BF16))
            d["xo"] = self.dout("xo", [NT, D])
        if self.debug:
            d["dbg"] = self.dout("dbg", [5, 256, NT])

        base = 16512
        S.arena("main", base)
        sb = S.sb
        t = self.t = {}
        t["x"] = sb("x", [128, NTT, D], F32)
        t["hT"] = sb("hT", [128, 8, 512], BF16)
        t["obn"] = sb("obn", [128, NTT, 256], BF16)
        t["mkT"] = sb("mkT", [128, 2, 256], BF16)
        t["mv"] = sb("mv", [128, 2, 4, 65], BF16)
        t["ident"] = sb("ident", [128, 128], BF16)
        t["bones64"] = sb("bones64", [128, 128], BF16)
        t["bones32"] = sb("bones32", [64, 64], BF16)
        t["colp"] = sb("colp", [128, 64], F32)
        t["gbc"] = sb("gbc", [128, D], F32)
        t["neglam"] = sb("neglam", [128, 1], F32)
        t["ss4"] = sb("ss4", [128, 4], F32)
        t["hb"] = sb("hb", [128, D], BF16)
        t["sq"] = sb("sq", [128, 512], BF16)
        t["std"] = sb("std", [128, 512], F32)
        NRING = 3
        t["ring"] = [sb(f"ring{i}", [128, 2048], BF16) for i in range(NRING)]
        common_end = S.sb_off["main"]
        S.arena("PB", common_end)
        t["bqT"] = sb("bqT", [64, 4, NT], BF16, "PB")
        t["bkT"] = [sb(f"bkT{i}", [64, 4096], BF16, "PB") for i in range(2)]
        t["bv"] = [sb(f"bv{i}", [128, 32, 65], BF16, "PB") for i in range(2)]
        t["U"] = sb("U", [128, 512], F32, "PB")
        t["Rt"] = sb("Rt", [128, 896], F32, "PB")
        t["scB"] = sb("scB", [128, 512], F32, "PB")
        t["biasB"] = sb("biasB", [128, 512], F32, "PB")
        t["sc2B"] = sb("sc2B", [128, 128], F32, "PB")
        t["btmp"] = [sb(f"btmp{i}", [128, 512], F32, "PB") for i in range(2)]
        t["bpt"] = [sb(f"bpt{i}", [128, 512], BF16, "PB") for i in range(3)]
        t["bo"] = sb("bo", [128, 2, 4, 64], F32, "PB")
        t["bob"] = sb("bob", [128, 4, 64], F32, "PB")
        t["bsq"] = sb("bsq", [128, 4, 64], F32, "PB")
        t["brd"] = sb("brd", [128, 4], F32, "PB")
        t["vtok"] = sb("vtok", [128, 768], BF16, "PB")
        t["ktmp"] = sb("ktmp", [128, 512], BF16, "PB")
        t["hmT"] = sb("hmT", [128, 8, 256], BF16, "PB")
        t["mgbc"] = sb("mgbc", [128, D], F32, "PB")
        t["xm"] = sb("xm", [128, 2, D], F32, "PB")
        t["lamv"] = sb("lamv", [128, 128], F32, "PB")
        t["lamt"] = sb("lamt", [128, 8], F32, "PB")
        S.arena("G", common_end)
        t["akT"] = sb("akT", [128, 2, 12 * 128], BF16, "G")
        t["av"] = sb("av", [128, 12, 4, 65], BF16, "G")
        t["bval"] = sb("bval", [128, 7, 4, 128], F32, "G")
        t["rowvalid"] = sb("rowvalid", [128, NTT * A_MAXSLOT * 2], F32, "G")
        t["cx"] = sb("cx", [128, 6, 256], BF16, "G")
        t["avh"] = sb("avh", [128, 2, 256], BF16, "G")
        t["ytT"] = sb("ytT", [128, 2, 512], BF16, "G")
        t["pwT"] = sb("pwT", [128, 7, 4, 128], BF16, "G")
        t["dwsT"] = sb("dwsT", [128, 4, 128], BF16, "G")
        t["cw"] = sb("cw", [64, 4, 64], BF16, "G")
        t["lnp"] = sb("lnp", [128, 512], F32, "G")
        t["qaT"] = sb("qaT", [128, 4, 512], BF16, "G")
        t["mqT"] = sb("mqT", [128, 4, 512], BF16, "G")
        t["zT"] = {k: sb(f"zT{k}", [128, 2, 512], BF16, "G") for k in "abm"}
        t["zTc"] = sb("zTc", [64, 4, 512], BF16, "G")
        t["yT"] = {k: t["zT"][k] for k in "abm"}
        t["yT"]["d"] = sb("yTd", [128, 2, 512], BF16, "G")
        t["yTc"] = t["zTc"]
        t["macc"] = sb("macc", [128, 4, 512], F32, "G")
        t["mbf"] = sb("mbf", [128, 8, 512], BF16, "G")
        t["sig"] = [sb(f"sig{i}", [128, 512], F32, "G") for i in range(1)]
        t["atmp"] = [sb(f"atmp{i}", [128, 512], F32, "G") for i in range(2)]
        t["apt"] = [sb(f"apt{i}", [128, 4, 128], BF16, "G") for i in range(2)]
        t["ytok"] = sb("ytok", [128, 4, 256], BF16, "G")
        t["ard"] = sb("ard", [128, 4], F32, "G")
        t["dl"] = sb("dl", [64, 512], BF16, "G")
        t["dg"] = sb("dg", [128, 768], F32, "G")
        t["atmp0"] = t["atmp"][0]
        t["atmp1"] = t["atmp"][1]
        t["dst"] = sb("dst", [128, 8], F32, "G")
        t["dvn"] = sb("dvn", [128, 256], BF16, "G")
        t["dy"] = sb("dy", [128, 256], F32, "G")
        t["ytok"] = t["ytok"]
        t["wbuf"] = [sb(f"wbuf{i}", [128, 2048], BF16, "G") for i in range(2)]
        assert S.sb_hi <= 229344, S.sb_hi
        self.sb_hi = S.sb_hi

        p = self.p = {}
        for nm in ("pjA", "pjB", "sA", "sB", "avA", "avB", "yp"):
            p[nm] = S.ps(nm, [128, 512], F32)
        p["tp"] = S.ps("tp", [128, 1024], BF16)

        self.load_consts()
        self.load_x()
        for li in range(nl):
            self.layer(li)
        if mode in ("M", "F", "MG"):
            keys = self.store_x()
            S.finish(outputs=keys)
        elif mode == "MB":
            S.dma("sp", d["obn_o"], t["obn"][:].rearrange("p a b -> p (a b)"), r=["obn"], w=["obn_out"])
            S.finish(outputs=["obn_out"])
        else:
            S.finish(outputs=["kt_own_d", "v_own_d"])

    def load_consts(self):
        S, t, d = self.S, self.t, self.d
        S.dma("pool", t["ident"][:], d["ident"], w=["ident"])
        S.dma("pool", t["bones64"][:], d["bones64"], w=["bones64"])
        S.dma("pool", t["bones32"][:], d["bones32"], w=["bones32"])

    def load_x(self):
        S, t, d = self.S, self.t, self.d
        xv = d["x"].rearrange("(t p) c -> p t c", p=128)
        for q in range(4):
            S.dma("sp", t["x"][:, 4 * q:4 * q + 4, :], xv[:, 4 * q:4 * q + 4, :],
                  w=[f"x{tt}" for tt in range(4 * q, 4 * q + 4)])

    def store_x(self):
        S, t, d = self.S, self.t, self.d
        xv = d["xo"].rearrange("(t p) c -> p t c", p=128)
        keys = []
        for q in range(4):
            S.dma("sp", xv[:, 4 * q:4 * q + 4, :], t["x"][:, 4 * q:4 * q + 4, :],
                  r=[f"x{tt}" for tt in range(4 * q, 4 * q + 4)], w=[f"xo{q}"])
            keys.append(f"xo{q}")
        return keys

    def wstream_begin(self, items):
        self.ws_items = items
        self.ws_next = 0
        self.ws_hold = 0

    def wget(self, i):
        S, t = self.S, self.t
        ring = t["ring"]
        R = len(ring)
        lim = getattr(self, "ws_hold", i) + R - 1
        while self.ws_next < len(self.ws_items) and self.ws_next <= lim:
            k = self.ws_next
            src, view = self.ws_items[k]
            slot = ring[k % R]
            S.dma("pool", view(slot), src, w=[f"ring{k % R}"])
            self.ws_next += 1
        assert self.ws_next > i
        slot = ring[i % R]
        return self.ws_items[i][1](slot), f"ring{i % R}"

    def wgroup(self, i, k):
        self.ws_hold = i
        out = [self.wget(i + a) for a in range(k)]
        return out

    def layer(self, li):
        S, t, d, p = self.S, self.t, self.d, self.p
        self.li = li
        if self.mode == "F":
            for k_ in ("kt_own", "v_own", "kt_all", "v_all"):
                d[k_] = self.dl[k_][li]
        self.layer_tables(li)
        self.mem_kv(li)
        w_in = d["w_in"][li]

        def wsrc(c0, n):
            return w_in[:, c0:c0 + n].rearrange("(c p) n -> p c n", p=128)

        def v256(slot):
            return slot[:, 0:2048].rearrange("p (c n) -> p c n", c=8)

        pcols = [C_AK, C_BK, C_BQ, C_AV, C_BV, C_CX]
        items = []
        for G in range(4):
            for c0 in pcols:
                items.append((wsrc(c0, 256), v256))
        self.wstream_begin(items)
        wi = 0
        for G in range(4):
            self.emit_hT(G)
            (wa, ka), = self.wgroup(wi, 1); wi += 1
            self.p_ak(G, wa, ka)
            (wa, ka), = self.wgroup(wi, 1); wi += 1
            self.p_b64(G, wa, ka, which="k")
            (wa, ka), = self.wgroup(wi, 1); wi += 1
            self.p_b64(G, wa, ka, which="q")
            w3 = self.wgroup(wi, 3); wi += 3
            self.p_vtok(G, w3)
        if self.mode == "P":
            return
        self.exchange(li)
        import os
        dbg = os.environ.get("KDBG", "")
        if "nob" not in dbg and self.mode != "MG":
            self.b_phase(li)
        if self.mode == "MG" and os.environ.get("NOOBN") != "1":
            S.dma("sp", t["obn"][:].rearrange("p a b -> p (a b)"), d["obn_i"], w=["obn"])
        S.barrier()
        if "nog" not in dbg and self.mode != "MB":
            self.g_phase(li)
        S.barrier()

    def layer_tables(self, li):
        S, t, d = self.S, self.t, self.d
        S.dma("sp", t["colp"][:], d["colp"][li], w=["colp"])
        S.dma("sp", t["gbc"][:], d["rowp"][li][:, 0:1024].broadcast_to([128, 1024]), w=["gbc"])
        S.dma("sp", t["mgbc"][:], d["rowp"][li][:, 1024:2048].broadcast_to([128, 1024]), w=["mgbc"])
        S.dma("sp", t["lamv"][:], d["lamp"][li].broadcast_to([128, 128]), w=["lamv"])
        lamv, lamt = t["lamv"], t["lamt"]
        S.op("dve", lambda e: e.tensor_tensor(out=lamv[:, 0:32], in0=lamv[:, 0:32], in1=lamv[:, 32:64], op=ALU.mult),
             r=["lamv"], w=["lamv"])
        S.op("dve", lambda e: e.tensor_tensor(out=lamv[:, 64:96], in0=lamv[:, 64:96], in1=lamv[:, 96:128], op=ALU.mult),
             r=["lamv"], w=["lamv"])
        S.op("dve", lambda e: e.tensor_reduce(out=lamt[:, 0:1], in_=lamv[:, 0:32], axis=AX.X, op=ALU.add),
             r=["lamv"], w=["lamt"])
        S.op("dve", lambda e: e.tensor_reduce(out=lamt[:, 1:2], in_=lamv[:, 64:96], axis=AX.X, op=ALU.add),
             r=["lamv"], w=["lamt"])
        S.op("act", lambda e: e.activation(out=lamt[:, 2:4], in_=lamt[:, 0:2], func=AF.Exp), r=["lamt"], w=["lamt"])
        S.op("dve", lambda e: e.tensor_tensor(out=lamt[:, 4:5], in0=lamt[:, 3:4], in1=lamt[:, 2:3], op=ALU.subtract),
             r=["lamt"], w=["lamt"])
        S.op("dve", lambda e: e.tensor_tensor(out=t["neglam"][:], in0=lamt[:, 4:5], in1=t["colp"][:, 57:58],
                                              op=ALU.subtract), r=["lamt", "colp"], w=["neglam"])

    def rms_rows(self, src_ap_fn, src_keys, ntile, gbc_key, gbc, hb_cb):
        S, t = self.S, self.t
        ss4, hb = t["ss4"], t["hb"]
        junk = hb
        for j in range(ntile):
            S.op("act", lambda e, j=j: e.activation(out=junk[:], in_=src_ap_fn(j), func=AF.Square,
                                                    accum_out=ss4[:, j:j + 1]),
                 r=[src_keys[j]], w=["hb", "ss4"])
        S.op("act", lambda e: e.activation(out=ss4[:, 0:ntile], in_=ss4[:, 0:ntile], func=AF.Sqrt,
                                           bias=EPS, scale=1.0 / D), r=["ss4"], w=["ss4"])
        S.op("dve", lambda e: e.reciprocal(ss4[:, 0:ntile], ss4[:, 0:ntile]), r=["ss4"], w=["ss4"])
        for j in range(ntile):
            S.op("dve", lambda e, j=j: e.scalar_tensor_tensor(out=hb[:], in0=src_ap_fn(j), scalar=ss4[:, j:j + 1],
                                                              in1=gbc[:], op0=ALU.mult, op1=ALU.mult),
                 r=[src_keys[j], "ss4", gbc_key], w=["hb"])
            hb_cb(j)

    def transpose_hb(self, dst_ap, dst_key):
        S, t, p = self.S, self.t, self.p
        hb, tp, ident = t["hb"], p["tp"], t["ident"]

        def tr(e):
            for c in range(8):
                i = e.transpose(tp[:, c * 128:(c + 1) * 128], hb[:, c * 128:(c + 1) * 128], ident[:])
            return i
        S.op("pe", tr, r=["hb", "ident"], w=["tp"])
        S.op("act", lambda e: e.activation(out=dst_ap, in_=tp[:].rearrange("p (c n) -> p c n", c=8), func=AF.Copy),
             r=[], w=["tp", dst_key])

    def emit_hT(self, G):
        t = self.t
        x, hT = t["x"], t["hT"]
        self.rms_rows(lambda j: x[:, 4 * G + j, :], [f"x{4 * G + j}" for j in range(4)], 4, "gbc", t["gbc"],
                      lambda j: self.transpose_hb(hT[:, :, j * 128:(j + 1) * 128], "hT"))

    def proj_fm(self, bank, w_ap, wkey, c0, m, rhs_fn, rkey, n):
        S, p = self.S, self.p
        ps = p[bank]

        def mm(e):
            for c in range(8):
                i = e.matmul(ps[0:m, 0:n], w_ap[:, c, c0:c0 + m], rhs_fn(c), start=(c == 0), stop=(c == 7))
            return i
        S.op("pe", mm, r=[wkey, rkey], w=[bank])

    def headnorm(self, bank, sbank, m, n, bones, bones_key, sq_scale, sq_bias, gcol, out_ap, out_key):
        S, t, p = self.S, self.t, self.p
        ps, ps2, sq, std = p[bank], p[sbank], t["sq"], t["std"]
        S.op("act", lambda e: e.activation(out=sq[0:m, 0:n], in_=ps[0:m, 0:n], func=AF.Square), r=[], w=[bank, "sq"])
        S.op("pe", lambda e: e.matmul(ps2[0:m, 0:n], bones, sq[0:m, 0:n], start=True, stop=True),
             r=["sq", bones_key], w=[sbank])
        S.op("act", lambda e: e.activation(out=std[0:m, 0:n], in_=ps2[0:m, 0:n], func=AF.Sqrt, bias=sq_bias,
                                           scale=sq_scale), r=[], w=[sbank, "std"])
        S.op("dve", lambda e: e.reciprocal(std[0:m, 0:n], std[0:m, 0:n]), r=["std"], w=["std"])
        gcols = gcol if isinstance(gcol, list) else [gcol]
        outs = out_ap if isinstance(out_ap, list) else [out_ap]
        for gc_, oa_ in zip(gcols, outs):
            S.op("dve", lambda e, gc_=gc_, oa_=oa_: e.scalar_tensor_tensor(out=oa_, in0=ps[0:m, 0:n], scalar=gc_,
                                                                         in1=std[0:m, 0:n], op0=ALU.mult, op1=ALU.mult),
                 r=["std", "colp"], w=[bank, out_key])

    def p_ak(self, G, w_ap, wkey):
        S, t, d = self.S, self.t, self.d
        hT = t["hT"]
        for ch in range(2):
            bank = "pjA" if ch == 0 else "pjB"
            self.proj_fm(bank, w_ap, wkey, ch * 128, 128, lambda c: hT[:, c, :], "hT", 512)
            self.headnorm(bank, "sA", 128, 512, t["bones64"][:], "bones64", 1.0 / 64, EPS,
                          t["colp"][:, 41:42], t["ktmp"][:], "ktmp")
            S.dma("sp", d["kt_own"][ch * 128:(ch + 1) * 128, G * 512:(G + 1) * 512], t["ktmp"][:],
                  r=["ktmp"], w=["kt_own_d"])

    def p_b64(self, G, w_ap, wkey, which):
        S, t, d = self.S, self.t, self.d
        hT = t["hT"]
        for h in range(4):
            bank = "pjA" if h % 2 == 0 else "pjB"
            self.proj_fm(bank, w_ap, wkey, h * 64, 64, lambda c: hT[:, c, :], "hT", 512)
            if which == "k":
                self.headnorm(bank, "sA", 64, 512, t["bones32"][:], "bones32", 1.0 / 32, EPS,
                              t["colp"][0:64, 45:46], t["ktmp"][0:64, :], "ktmp")
                S.dma("sp", d["kt_own"][256 + h * 64:256 + (h + 1) * 64, G * 512:(G + 1) * 512],
                      t["ktmp"][0:64, :], r=["ktmp"], w=["kt_own_d"])
            else:
                self.headnorm(bank, "sA", 64, 512, t["bones32"][:], "bones32", 1.0, 32 * EPS,
                              t["colp"][0:64, 44:45], t["bqT"][:, h, G * 512:(G + 1) * 512], "bqT")

    def p_vtok(self, G, w3):
        S, t, d, p = self.S, self.t, self.d, self.p
        hT, vtok = t["hT"], t["vtok"]
        for j in range(4):
            tt = 4 * G + j
            for half, bank in ((0, "pjA"), (1, "pjB")):
                blocks = [0, 1] if half == 0 else [2]
                ps = p[bank]

                def mm(e, blocks=blocks, ps=ps, j=j):
                    for bi, b in enumerate(blocks):
                        wa = w3[b][0]
                        for c in range(8):
                            i = e.matmul(ps[:, bi * 256:(bi + 1) * 256], hT[:, c, j * 128:(j + 1) * 128],
                                         wa[:, c, :], start=(c == 0), stop=(c == 7))
                    return i
                S.op("pe", mm, r=["hT"] + [w3[b][1] for b in blocks], w=[bank])
                n = 256 * len(blocks)
                S.op("act", lambda e, ps=ps, n=n, half=half: e.activation(out=vtok[:, half * 512:half * 512 + n],
                                                                          in_=ps[:, 0:n], func=AF.Copy),
                     r=[], w=[bank, "vtok"])
            S.dma("sp", d["v_own"][tt * 128:(tt + 1) * 128, :], vtok[:], r=["vtok"], w=["v_own_d"])

    def mem_kv(self, li):
        S, t, d, p = self.S, self.t, self.d, self.p
        xm, hmT = t["xm"], t["hmT"]
        S.dma("sp", xm[:], d["mem"].rearrange("(t p) c -> p t c", p=128), w=["xm"])
        self.rms_rows(lambda j: xm[:, j, :], ["xm", "xm"], 2, "mgbc", t["mgbc"],
                      lambda j: self.transpose_hb(hmT[:, :, j * 128:(j + 1) * 128], "hmT"))
        wk = d["m_wkv"][li]
        ring = t["ring"]
        views = []
        for b in range(2):
            v = ring[b][:, 0:2048].rearrange("p (c n) -> p c n", c=8)
            S.dma("pool", v, wk[:, b * 256:(b + 1) * 256].rearrange("(c p) n -> p c n", p=128), w=[f"ring{b}"])
            views.append(v)
        for ch in range(2):
            bank = "pjA" if ch == 0 else "pjB"
            self.proj_fm(bank, views[0], "ring0", ch * 128, 128, lambda c: hmT[:, c, :], "hmT", 256)
            self.headnorm(bank, "sA", 128, 256, t["bones64"][:], "bones64", 1.0 / 64, EPS,
                          t["colp"][:, 43:44], t["mkT"][:, ch, :], "mkT")
        mv = t["mv"]
        S.op("pool", lambda e: e.memset(mv[:, :, :, 64:65], 1.0), r=[], w=["mv"])
        for j in range(2):
            ps = p["pjA"]

            def mm(e, j=j, ps=ps):
                for c in range(8):
                    i = e.matmul(ps[:, 0:256], hmT[:, c, j * 128:(j + 1) * 128], views[1][:, c, :],
                                 start=(c == 0), stop=(c == 7))
                return i
            S.op("pe", mm, r=["hmT", "ring1"], w=["pjA"])
            S.op("act", lambda e, j=j, ps=ps: e.activation(out=mv[:, j, :, 0:64],
                                                           in_=ps[:, 0:256].rearrange("p (h e) -> p h e", h=4),
                                                           func=AF.Copy), r=[], w=["pjA", "mv"])

    def exchange(self, li):
        if self.mode in ("M", "MB", "MG"):
            return
        S, d = self.S, self.d
        groups = [[0, 1], [2, 3], [4, 5], [6, 7]]
        kt_own, v_own = d["kt_own"], d["v_own"]
        kt_all2 = d["kt_all"].rearrange("r a b -> (r a) b")
        v_all2 = d["v_all"].rearrange("r a b -> (r a) b")

        def cc1(e):
            return [e.collective_compute("AllGather", ALU.bypass, replica_groups=groups, ins=[kt_own], outs=[kt_all2])]

        def cc2(e):
            return [e.collective_compute("AllGather", ALU.bypass, replica_groups=groups, ins=[v_own], outs=[v_all2])]
        S.dma_multi("pool", cc1, 1, r=["kt_own_d"], w=["kt_all"])
        S.dma_multi("pool", cc2, 1, r=["v_own_d"], w=["v_all"])

    def b_phase(self, li):
        S, t, d, p = self.S, self.t, self.d, self.p
        for nm in ("U", "Rt", "scB", "biasB", "sc2B"):
            S.dma("sp", t[nm][:], d[nm], w=[nm])
        kt_all, v_all = d["kt_all"], d["v_all"]
        bqT, obn = t["bqT"], t["obn"]
        U, Rt, scB, biasB, sc2B = t["U"], t["Rt"], t["scB"], t["biasB"], t["sc2B"]
        for i in range(2):
            S.op("pool", lambda e, i=i: e.memset(t["bv"][i][:, :, 64:65], 1.0), r=[], w=[f"bv{i}"])
        sidx = 0
        aidx = 0
        tix = 0
        pix = 0
        for h in range(4):
            bk, bkk = t["bkT"][h % 2], f"bkT{h % 2}"
            bvv, bvk = t["bv"][h % 2], f"bv{h % 2}"
            for r in range(2):
                S.dma("sp", bk[:, r * NT:(r + 1) * NT], kt_all[r, 256 + h * 64:256 + (h + 1) * 64, :],
                      r=["kt_all"], w=[bkk])
                S.dma("sp", bvv[:, r * 16:(r + 1) * 16, 0:64],
                      v_all[r, :, 256 + h * 64:256 + (h + 1) * 64].rearrange("(t p) e -> p t e", p=128),
                      r=["v_all"], w=[bvk])
            for qb in range(4):
                cands = b_cands(qb)
                for m in range(2):
                    avb = "avA" if aidx % 2 == 0 else "avB"
                    aidx += 1
                    av = p[avb]
                    for kt in range(32):
                        sb_ = "sA" if sidx % 2 == 0 else "sB"
                        sidx += 1
                        sps = p[sb_]
                        S.op("pe", lambda e, sps=sps, bk=bk, kt=kt, m=m, qb=qb, h=h: e.matmul(
                            sps[:, :], bk[m * 32:(m + 1) * 32, kt * 128:(kt + 1) * 128],
                            bqT[m * 32:(m + 1) * 32, h, qb * 512:(qb + 1) * 512], start=True, stop=True),
                            r=[bkk, "bqT"], w=[sb_])
                        tmp, tk = t["btmp"][tix % 2], f"btmp{tix % 2}"
                        tix += 1
                        col = (qb * 32 + kt) * 4 + h
                        S.op("dve", lambda e, tmp=tmp, sps=sps, col=col: e.scalar_tensor_tensor(
                            out=tmp[:], in0=U[:], scalar=scB[:, col:col + 1], in1=sps[:, :], op0=ALU.mult,
                            op1=ALU.add), r=["U", "scB"], w=[sb_, tk])
                        if kt in cands:
                            ci = cands.index(kt)
                            w0 = 384 - 128 * (ci % 4)
                            c2 = (qb * 8 + ci) * 4 + h
                            S.op("dve", lambda e, tmp=tmp, w0=w0, c2=c2: e.scalar_tensor_tensor(
                                out=tmp[:], in0=Rt[:, w0:w0 + 512], scalar=sc2B[:, c2:c2 + 1], in1=tmp[:],
                                op0=ALU.mult, op1=ALU.add), r=["Rt", "sc2B"], w=[tk])
                        pt, pk = t["bpt"][pix % 3], f"bpt{pix % 3}"
                        pix += 1
                        S.op("act", lambda e, pt=pt, tmp=tmp, col=col: e.activation(
                            out=pt[:], in_=tmp[:], func=AF.Exp, bias=biasB[:, col:col + 1], scale=1.0),
                            r=[tk, "biasB"], w=[pk])

                        def avmm(e, pt=pt, av=av, bvv=bvv, kt=kt):
                            for j in range(4):
                                i = e.matmul(av[:, j * 65:(j + 1) * 65], pt[:, j * 128:(j + 1) * 128],
                                             bvv[:, kt, :], start=(kt == 0 and j == 0), stop=(kt == 31),
                                             skip_group_check=True)
                            return i
                        S.op("pe", avmm, r=[pk, bvk], w=[avb])
                    av3 = av[:, 0:260].rearrange("p (j e) -> p j e", j=4)
                    brd, bo = t["brd"], t["bo"]
                    S.op("dve", lambda e, av3=av3: e.reciprocal(brd[:].rearrange("p (j o) -> p j o", o=1),
                                                                av3[:, :, 64:65]), r=[], w=[avb, "brd"])
                    S.op("dve", lambda e, av3=av3, m=m: e.tensor_tensor(
                        out=bo[:, m, :, :], in0=av3[:, :, 0:64],
                        in1=brd[:].rearrange("p (j o) -> p j o", o=1).broadcast_to([128, 4, 64]), op=ALU.mult),
                        r=["brd"], w=[avb, "bo"])
                bo, bob, bsq, brd = t["bo"], t["bob"], t["bsq"], t["brd"]
                S.op("dve", lambda e: e.scalar_tensor_tensor(out=bob[:], in0=bo[:, 1, :, :], scalar=t["neglam"][:, 0:1],
                                                             in1=bo[:, 0, :, :], op0=ALU.mult, op1=ALU.add),
                     r=["bo", "neglam"], w=["bob"])
                S.op("act", lambda e: e.activation(out=bsq[:], in_=bob[:], func=AF.Square), r=["bob"], w=["bsq"])
                S.op("dve", lambda e: e.tensor_reduce(out=brd[:], in_=bsq[:], axis=AX.X, op=ALU.add),
                     r=["bsq"], w=["brd"])
                S.op("act", lambda e: e.activation(out=brd[:], in_=brd[:], func=AF.Sqrt, bias=t["colp"][:, 56:57],
                                                   scale=t["colp"][:, 55:56]), r=["brd", "colp"], w=["brd"])
                S.op("dve", lambda e: e.reciprocal(brd[:], brd[:]), r=["brd"], w=["brd"])
                S.op("dve", lambda e, qb=qb, h=h: e.tensor_tensor(
                    out=obn[:, 4 * qb:4 * qb + 4, h * 64:(h + 1) * 64], in0=bob[:],
                    in1=brd[:].rearrange("p (j o) -> p j o", o=1).broadcast_to([128, 4, 64]), op=ALU.mult),
                    r=["bob", "brd"], w=["obn"])

    def g_phase(self, li):
        S, t, d, p = self.S, self.t, self.d, self.p
        kt_all, v_all, kt_own, v_own = d["kt_all"], d["v_all"], d["kt_own"], d["v_own"]
        S.dma("sp", t["bval"][:].rearrange("p a b c -> p (a b c)"), d["bval"][li], w=["bval"])
        S.dma("sp", t["rowvalid"][:], d["rowvalid"], w=["rowvalid"])
        S.dma("pool", t["pwT"][:].rearrange("p a b c -> p (a b c)"), d["pwT"], w=["pwT"])
        S.dma("pool", t["dwsT"][:].rearrange("p a b -> p (a b)"), d["dwsT"][li], w=["dwsT"])
        S.dma("pool", t["cw"][:].rearrange("p a b -> p (a b)"), d["cw"][li], w=["cw"])
        S.dma("sp", t["lnp"][:], d["rowp"][li][:, 2048:2560].broadcast_to([128, 512]), w=["lnp"])
        akT, av, cx = t["akT"], t["av"], t["cx"]
        S.op("pool", lambda e: e.memset(av[:, :, :, 64:65], 1.0), r=[], w=[f"av{a}" for a in range(12)])

        w_in = d["w_in"][li]

        def wsrc(c0, n):
            return w_in[:, c0:c0 + n].rearrange("(c p) n -> p c n", p=128)

        def v256(slot):
            return slot[:, 0:2048].rearrange("p (c n) -> p c n", c=8)

        wb, wo = d["w_branch"][li], d["w_out"][li]
        self.wb_n = 0

        def load_wb(i, half):
            n = self.wb_n
            self.wb_n += 1
            buf, key = t["wbuf"][n % 2], f"wbuf{n % 2}"
            c0 = half * 512
            if i == 2:
                v = buf[0:64, 0:2048].rearrange("p (g n) -> p g n", g=4)
                S.dma("pool", v, wb[2][:, c0:c0 + 512].rearrange("(g p) n -> p g n", p=64), w=[key])
            else:
                v = buf[:, 0:1024].rearrange("p (k n) -> p k n", k=2)
                S.dma("pool", v, wb[i][:, c0:c0 + 512].rearrange("(k p) n -> p k n", p=128), w=[key])
            return v, key

        slabs = [C_AQ, C_MQ, C_AZ, C_BZ, C_MZ, C_CZ, C_DU, C_DV, C_DZ]
        items = []
        for G in range(4):
            for c0 in slabs:
                items.append((wsrc(c0, 256), v256))
            for half in range(2):
                for i in range(5):
                    for q in range(2):
                        items.append((wsrc(C_GATE + i * 1024 + half * 512 + q * 256, 256), v256))
            for q in range(4):
                items.append((wo[:, q * 256:(q + 1) * 256].rearrange("(c p) n -> p c n", p=128), v256))
        self.wstream_begin(items)
        self.wi = 0

        def nextw():
            (r,) = self.wgroup(self.wi, 1)
            self.wi += 1
            return r

        for G in range(4):
            lo, hi = max(0, 4 * G - 3), min(15, 4 * G + 6)
            nown = hi - lo + 1
            self.a_lo, self.a_nown = lo, nown
            for ch in range(2):
                S.dma("sp", akT[:, ch, 0:nown * 128], kt_own[ch * 128:(ch + 1) * 128, lo * 128:(hi + 1) * 128],
                      r=["kt_own_d"], w=["akT"])
            for a in range(nown):
                S.dma("sp", av[:, a, :, 0:64],
                      v_own[(lo + a) * 128:(lo + a + 1) * 128, 0:256].rearrange("p (h e) -> p h e", h=4),
                      r=["v_own_d"], w=[f"av{a}"])
            if G in (0, 3):
                r_, c0 = (0, 1792) if G == 0 else (1, 0)
                for ch in range(2):
                    S.dma("sp", akT[:, ch, nown * 128:(nown + 2) * 128], kt_all[r_, ch * 128:(ch + 1) * 128, c0:c0 + 256],
                          r=["kt_all"], w=["akT"])
                avh = t["avh"]
                for a in range(2):
                    S.dma("sp", avh[:, a, :], v_all[r_, c0 + a * 128:c0 + (a + 1) * 128, 0:256], r=["v_all"], w=["avh"])
                    S.op("pool", lambda e, a=a, nown=nown: e.tensor_copy(
                        out=av[:, nown + a, :, 0:64], in_=avh[:, a, :].rearrange("p (h e) -> p h e", h=4)),
                        r=["avh"], w=[f"av{nown + a}"])
            clo, chi = max(0, 4 * G - 1), min(15, 4 * G + 4)
            s0 = clo - (4 * G - 1)
            S.dma("sp", cx[:, s0:s0 + (chi - clo + 1), :],
                  v_own[clo * 128:(chi + 1) * 128, 512:768].rearrange("(t p) c -> p t c", p=128), r=["v_own_d"], w=["cx"])
            if G == 0:
                S.dma("sp", cx[:, 0, :], v_all[0, 1920:2048, 512:768], r=["v_all"], w=["cx"])
            if G == 3:
                S.dma("sp", cx[:, 5, :], v_all[1, 0:128, 512:768], r=["v_all"], w=["cx"])

            self.emit_hT(G)
            hT = t["hT"]
            for (dst, gc, gm) in ((t["qaT"], 40, 58), (t["mqT"], 42, 60)):
                wa, wk = nextw()
                for ch in range(2):
                    bank = "pjA" if ch == 0 else "pjB"
                    self.proj_fm(bank, wa, wk, ch * 128, 128, lambda c: hT[:, c, :], "hT", 512)
                    self.headnorm(bank, "sA", 128, 512, t["bones64"][:], "bones64", 1.0, 64 * EPS,
                                  [t["colp"][:, gm:gm + 1], t["colp"][:, gm + 1:gm + 2]],
                                  [dst[:, 2 * ch, :], dst[:, 2 * ch + 1, :]], "q" + str(gc))
            for k in "abm":
                wa, wk = nextw()
                for ch in range(2):
                    bank = "pjA" if ch == 0 else "pjB"
                    self.proj_fm(bank, wa, wk, ch * 128, 128, lambda c: hT[:, c, :], "hT", 512)
                    S.op("act", lambda e, bank=bank, k=k, ch=ch: e.activation(out=t["zT"][k][:, ch, :], in_=p[bank][:, :],
                                                                              func=AF.Silu), r=[], w=[bank, "zT" + k])
            wa, wk = nextw()
            for g in range(4):
                bank = "pjA" if g % 2 == 0 else "pjB"
                self.proj_fm(bank, wa, wk, g * 64, 64, lambda c: hT[:, c, :], "hT", 512)
                S.op("act", lambda e, bank=bank, g=g: e.activation(out=t["zTc"][:, g, :], in_=p[bank][0:64, :],
                                                                    func=AF.Silu), r=[], w=[bank, "zTc"])
            import os
            gs = int(os.environ.get("GSTOP", "99"))
            if gs <= 1:
                break
            w3 = self.wgroup(self.wi, 3)
            self.wi += 3
            self.d_branch(G, w3)
            if gs <= 2:
                break
            if os.environ.get("SKIPA") != "1":
                self.a_branch(G)
            if gs <= 3:
                break
            self.m_branch(G)
            if gs <= 4:
                break
            self.c_branch(G)
            if gs <= 5:
                break
            self.b_finish(G)
            if gs <= 6:
                break
            order = [("a", 0), ("b", 1), ("c", 2), ("d", 3), ("m", 4)]
            ykey = {"a": "zTa", "b": "zTb", "m": "zTm", "d": "yTd"}
            seq = [(half, nm, i) for half in range(2) for (nm, i) in order]
            wb_next = load_wb(0, 0)
            for si, (half, nm, i) in enumerate(seq):
                wb_cur = wb_next
                if si + 1 < len(seq):
                    wb_next = load_wb(seq[si + 1][2], seq[si + 1][0])
                for q in range(2):
                    wg, wgk = nextw()
                    for dd in range(2):
                        dl_ = 2 * q + dd
                        dc = half * 4 + dl_
                        bank = "pjA" if dd == 0 else "pjB"
                        self.proj_fm(bank, wg, wgk, dd * 128, 128, lambda c: hT[:, c, :], "hT", 512)
                        sg, sgk = t["sig"][0], "sig0"
                        S.op("act", lambda e, bank=bank, sg=sg, i=i, dc=dc: e.activation(
                            out=sg[:], in_=p[bank][:, :], func=AF.Sigmoid,
                            bias=t["colp"][:, i * 8 + dc:i * 8 + dc + 1], scale=1.0),
                            r=["colp"], w=[bank, sgk])
                        yp = p["yp"]
                        wv, wvk = wb_cur
                        cc = dl_ * 128
                        if i == 2:
                            def mm(e, wv=wv, cc=cc):
                                for g in range(4):
                                    ii = e.matmul(yp[:, :], wv[:, g, cc:cc + 128], t["yTc"][:, g, :],
                                                  start=(g == 0), stop=(g == 3))
                                return ii
                            S.op("pe", mm, r=[wvk, "zTc"], w=["yp"])
                        else:
                            yT = t["yT"][nm]

                            def mm(e, wv=wv, yT=yT, cc=cc):
                                for k in range(2):
                                    ii = e.matmul(yp[:, :], wv[:, k, cc:cc + 128], yT[:, k, :],
                                                  start=(k == 0), stop=(k == 1))
                                return ii
                            S.op("pe", mm, r=[wvk, ykey[nm]], w=["yp"])
                        macc, mbf = t["macc"], t["mbf"]
                        mk_ = f"macc{dl_}"
                        if i == 0:
                            S.op("dve", lambda e, sg=sg, dl_=dl_: e.tensor_tensor(out=macc[:, dl_, :], in0=sg[:], in1=yp[:, :],
                                                                                  op=ALU.mult), r=[sgk], w=["yp", mk_])
                        else:
                            S.op("dve", lambda e, sg=sg: e.tensor_tensor(out=sg[:], in0=sg[:], in1=yp[:, :], op=ALU.mult),
                                 r=[], w=["yp", sgk])
                            if i < 4:
                                S.op("dve", lambda e, dl_=dl_, sg=sg: e.tensor_tensor(out=macc[:, dl_, :], in0=macc[:, dl_, :],
                                                                                      in1=sg[:], op=ALU.add),
                                     r=[sgk], w=[mk_])
                            else:
                                S.op("dve", lambda e, dl_=dl_, dc=dc, sg=sg: e.tensor_tensor(
                                    out=mbf[:, dc, :], in0=macc[:, dl_, :], in1=sg[:], op=ALU.add),
                                    r=[sgk, mk_], w=[f"mbf{dc}"])
            x = t["x"]
            for q in range(4):
                wv, wvk = nextw()
                for j in range(4):
                    tt = 4 * G + j
                    bank = "pjA" if j % 2 == 0 else "pjB"
                    ps = p[bank]

                    def mm(e, ps=ps, wv=wv, j=j):
                        for c in range(8):
                            ii = e.matmul(ps[:, 0:256], t["mbf"][:, c, j * 128:(j + 1) * 128], wv[:, c, :],
                                          start=(c == 0), stop=(c == 7))
                        return ii
                    S.op("pe", mm, r=[wvk] + [f"mbf{c}" for c in range(8)], w=[bank])
                    S.op("dve", lambda e, ps=ps, tt=tt, q=q: e.tensor_tensor(
                        out=x[:, tt, q * 256:(q + 1) * 256], in0=x[:, tt, q * 256:(q + 1) * 256], in1=ps[:, 0:256],
                        op=ALU.add), r=[], w=[bank, f"x{tt}"])

    def tok2fm(self, src_fn, src_key, zT, zkey, gcol, dstT, dkey, ntile=4):
        S, t, p = self.S, self.t, self.p
        tp, ident = p["tp"], t["ident"]

        def tr(e):
            for j in range(ntile):
                for kc in range(2):
                    i = e.transpose(tp[:, (kc * 4 + j) * 128:(kc * 4 + j + 1) * 128],
                                    src_fn(j)[:, kc * 128:(kc + 1) * 128], ident[:])
            return i
        S.op("pe", tr, r=[src_key, "ident"], w=["tp"])
        tpv = tp[:].rearrange("p (k n) -> p k n", k=2)
        ytT = t["ytT"]
        if gcol is None:
            S.op("act", lambda e: e.activation(out=ytT[:], in_=tpv, func=AF.Copy), r=[], w=["tp", "ytT"])
        else:
            S.op("act", lambda e: e.activation(out=ytT[:], in_=tpv, func=AF.Copy, scale=gcol), r=["colp"], w=["tp", "ytT"])
        S.op("dve", lambda e: e.tensor_tensor(out=dstT[:], in0=ytT[:], in1=zT[:], op=ALU.mult),
             r=["ytT"], w=[dkey])

    def b_finish(self, G):
        t = self.t
        obn = t["obn"]
        self.tok2fm(lambda j: obn[:, 4 * G + j, :], "obn", t["zT"]["b"], "zTb", t["colp"][:, 50:51], t["yT"]["b"], "zTb")

    def a_branch(self, G):
        S, t, p = self.S, self.t, self.p
        akT, av, qaT, bval, rv = t["akT"], t["av"], t["qaT"], t["bval"], t["rowvalid"]
        ytok, ard = t["ytok"], t["ard"]
        cnt = 0
        for j in range(4):
            tt = 4 * G + j
            avb = "avA" if j % 2 == 0 else "avB"
            avp = p[avb]
            slots = a_slots(tt)
            for si, (gslot, di) in enumerate(slots):
                slot = gslot - self.a_lo if gslot < 16 else self.a_nown + (gslot - 16) % 2
                assert 0 <= slot < 12
                sb_ = "sA" if cnt % 2 == 0 else "sB"
                sps = p[sb_]

                def qk(e, sps=sps, slot=slot, j=j):
                    for h in range(4):
                        i = e.matmul(sps[:, h * 128:(h + 1) * 128], akT[:, h // 2, slot * 128:(slot + 1) * 128],
                                     qaT[:, h, j * 128:(j + 1) * 128], start=True, stop=True,
                                     skip_group_check=True)
                    return i
                S.op("pe", qk, r=["akT", "q40"], w=[sb_])
                import os
                ast = int(os.environ.get("ASTOP", "9"))
                if ast <= 1:
                    cnt += 1
                    continue
                tmp, tk = t["atmp"][cnt % 2], f"atmp{cnt % 2}"
                S.op("dve", lambda e, tmp=tmp, sps=sps, di=di: e.tensor_tensor(
                    out=tmp[:], in0=sps[:, :], in1=bval[:, di, :, :].rearrange("p h q -> p (h q)"), op=ALU.add),
                    r=["bval"], w=[sb_, tk])
                if ast <= 2:
                    cnt += 1
                    continue
                pt, pk = t["apt"][cnt % 2], f"apt{cnt % 2}"
                tmp3 = tmp[:].rearrange("p (h q) -> p h q", h=4)
                for qr in range(2):
                    col = (tt * A_MAXSLOT + si) * 2 + qr
                    S.op("act", lambda e, pt=pt, tmp3=tmp3, qr=qr, col=col: e.activation(
                        out=pt[:, :, qr * 64:(qr + 1) * 64], in_=tmp3[:, :, qr * 64:(qr + 1) * 64], func=AF.Exp,
                        bias=rv[:, col:col + 1], scale=1.0), r=[tk, "rowvalid"], w=[pk])

                if ast <= 3:
                    cnt += 1
                    continue

                def avmm(e, pt=pt, avp=avp, slot=slot, si=si, last=(si == len(slots) - 1)):
                    for h in range(4):
                        i = e.matmul(avp[:, h * 65:(h + 1) * 65], pt[:, h, :], av[:, slot, h, :],
                                     start=(si == 0 and h == 0), stop=last, skip_group_check=True)
                    return i
                _m = os.environ.get("AVNODEP", "0")
                _r = [pk, f"av{slot}"]
                if _m == "1" or (_m == "3" and gslot >= 16) or (_m == "4" and gslot < 16):
                    _r = [pk]
                S.op("pe", avmm, r=_r, w=[avb])
                cnt += 1
            if int(os.environ.get("ASTOP", "9")) <= 4:
                continue
            av3 = avp[:, 0:260].rearrange("p (h e) -> p h e", h=4)
            S.op("dve", lambda e, av3=av3: e.reciprocal(ard[:].rearrange("p (h o) -> p h o", o=1), av3[:, :, 64:65]),
                 r=[], w=[avb, "ard"])
            S.op("dve", lambda e, av3=av3, j=j: e.tensor_tensor(
                out=ytok[:, j, :].rearrange("p (h e) -> p h e", h=4), in0=av3[:, :, 0:64],
                in1=ard[:].rearrange("p (h o) -> p h o", o=1).broadcast_to([128, 4, 64]), op=ALU.mult),
                r=["ard"], w=[avb, "ytok"])
        import os
        if int(os.environ.get("ASTOP", "9")) >= 6:
            self.tok2fm(lambda j: ytok[:, j, :], "ytok", t["zT"]["a"], "zTa", None, t["yT"]["a"], "zTa")

    def m_branch(self, G):
        S, t, p = self.S, self.t, self.p
        mkT, mv, mqT, ytok, ard = t["mkT"], t["mv"], t["mqT"], t["ytok"], t["ard"]
        cnt = 0
        for h in range(4):
            pb = (h % 2) * 64
            avb = "avA" if h % 2 == 0 else "avB"
            avp = p[avb]
            for kt in range(2):
                sb_ = "sA" if cnt % 2 == 0 else "sB"
                sps = p[sb_]
                S.op("pe", lambda e, sps=sps, pb=pb, h=h, kt=kt: e.matmul(
                    sps[:, :], mkT[:, h // 2, kt * 128:(kt + 1) * 128], mqT[:, h, :],
                    start=True, stop=True), r=["mkT", "q42"], w=[sb_])
                ptb = t["apt"][cnt % 2][:].rearrange("p h q -> p (h q)")
                pkb = f"apt{cnt % 2}"
                S.op("act", lambda e, ptb=ptb, sps=sps: e.activation(out=ptb, in_=sps[:, :], func=AF.Exp),
                     r=[], w=[sb_, pkb])

                def avmm(e, ptb=ptb, avp=avp, kt=kt, h=h):
                    for j in range(4):
                        i = e.matmul(avp[:, j * 65:(j + 1) * 65], ptb[:, j * 128:(j + 1) * 128], mv[:, kt, h, :],
                                     start=(kt == 0 and j == 0), stop=(kt == 1), skip_group_check=True)
                    return i
                S.op("pe", avmm, r=[pkb, "mv"], w=[avb])
                cnt += 1
            av3 = avp[:, 0:260].rearrange("p (j e) -> p j e", j=4)
            S.op("dve", lambda e, av3=av3: e.reciprocal(ard[:].rearrange("p (h o) -> p h o", o=1), av3[:, :, 64:65]),
                 r=[], w=[avb, "ard"])
            S.op("dve", lambda e, av3=av3, h=h: e.tensor_tensor(
                out=ytok[:, :, h * 64:(h + 1) * 64], in0=av3[:, :, 0:64],
                in1=ard[:].rearrange("p (h o) -> p h o", o=1).broadcast_to([128, 4, 64]), op=ALU.mult),
                r=["ard"], w=[avb, "ytok"])
        self.tok2fm(lambda j: ytok[:, j, :], "ytok", t["zT"]["m"], "zTm", None, t["yT"]["m"], "zTm")

    def c_branch(self, G):
        S, t, p = self.S, self.t, self.p
        cx, pwT, cw, dl, yTc, zTc = t["cx"], t["pwT"], t["cw"], t["dl"], t["yTc"], t["zTc"]
        for g in range(4):
            bank = "sA" if g % 2 == 0 else "sB"
            ps = p[bank]

            def mm(e, g=g, ps=ps):
                first = True
                for j in range(4):
                    tt = 4 * G + j
                    var = [0, 1, 2]
                    if tt == 0:
                        var = [3, 4, 2]
                    if tt == 15:
                        var = [0, 5, 6]
                    for k in range(3):
                        i = e.matmul(ps[0:64, j * 128:(j + 1) * 128], cx[:, j + k, g * 64:(g + 1) * 64],
                                     pwT[:, var[k], g, :], start=first, stop=(k == 2), skip_group_check=True)
                        first = False
                return i
            S.op("pe", mm, r=["cx", "pwT"], w=[bank])
            S.op("act", lambda e, ps=ps: e.activation(out=dl[:], in_=ps[0:64, :], func=AF.Copy), r=[], w=[bank, "dl"])
            yp = p["yp"]
            S.op("pe", lambda e, g=g: e.matmul(yp[0:64, :], cw[:, g, :], dl[:], start=True, stop=True),
                 r=["cw", "dl"], w=["yp"])
            S.op("dve", lambda e, g=g: e.scalar_tensor_tensor(out=yTc[:, g, :], in0=yp[0:64, :],
                                                              scalar=t["colp"][0:64, 46 + g:47 + g], in1=zTc[:, g, :],
                                                              op0=ALU.mult, op1=ALU.mult),
                 r=["colp"], w=["yp", "zTc"])

    def d_branch(self, G, w3):
        S, t, p = self.S, self.t, self.p
        hT, dg, dt1, dt2, dst, dvn, dy, lnp, dwsT = (t["hT"], t["dg"], t["atmp0"], t["atmp1"], t["dst"], t["dvn"],
                                                     t["dy"], t["lnp"], t["dwsT"])
        ytok = t["ytok"]
        for j in range(4):
            for half, bank in ((0, "pjA"), (1, "pjB")):
                blocks = [0, 1] if half == 0 else [2]
                ps = p[bank]

                def mm(e, blocks=blocks, ps=ps, j=j):
                    for bi, b in enumerate(blocks):
                        wa = w3[b][0]
                        for c in range(8):
                            i = e.matmul(ps[:, bi * 256:(bi + 1) * 256], hT[:, c, j * 128:(j + 1) * 128],
                                         wa[:, c, :], start=(c == 0), stop=(c == 7))
                    return i
                S.op("pe", mm, r=["hT"] + [w3[b][1] for b in blocks], w=[bank])
            pu, pz = p["pjA"], p["pjB"]
            S.op("act", lambda e: e.activation(out=dt1[:], in_=pu[:, :], func=AF.Square), r=[], w=["pjA", "atmp0"])
            S.op("dve", lambda e: e.tensor_scalar(out=dt1[:], in0=dt1[:], scalar1=0.044715, scalar2=1.0, op0=ALU.mult,
                                                  op1=ALU.add), r=["atmp0"], w=["atmp0"])
            S.op("dve", lambda e: e.tensor_tensor(out=dt1[:], in0=dt1[:], in1=pu[:, :], op=ALU.mult), r=["atmp0"],
                 w=["pjA", "atmp0"])
            S.op("act", lambda e: e.activation(out=dt2[:], in_=dt1[:], func=AF.Sigmoid, scale=1.5957691216057308),
                 r=["atmp0"], w=["atmp1"])
            S.op("dve", lambda e: e.tensor_tensor(out=dg[:, 0:512], in0=dt2[:], in1=pu[:, :], op=ALU.mult), r=["atmp1"],
                 w=["pjA", "dg"])
            S.op("act", lambda e: e.activation(out=dg[:, 512:768], in_=pz[:, 0:256], func=AF.Silu), r=[], w=["pjB", "dg"])
            S.op("dve", lambda e: e.bn_stats(out=dst[:, 0:6], in_=dg[:, 256:512]), r=["dg"], w=["dst"])
            S.op("dve", lambda e: e.bn_aggr(out=dst[:, 6:8], in_=dst[:, 0:6]), r=["dst"], w=["dst"])
            S.op("act", lambda e: e.activation(out=dst[:, 7:8], in_=dst[:, 7:8], func=AF.Sqrt, bias=EPS, scale=1.0),
                 r=["dst"], w=["dst"])
            S.op("dve", lambda e: e.reciprocal(dst[:, 7:8], dst[:, 7:8]), r=["dst"], w=["dst"])
            S.op("dve", lambda e: e.tensor_scalar(out=dt1[:, 0:256], in0=dg[:, 256:512], scalar1=dst[:, 6:7],
                                                  scalar2=dst[:, 7:8], op0=ALU.subtract, op1=ALU.mult),
                 r=["dg", "dst"], w=["atmp0"])
            S.op("dve", lambda e: e.tensor_tensor(out=dt1[:, 0:256], in0=dt1[:, 0:256], in1=lnp[:, 0:256], op=ALU.mult),
                 r=["lnp", "atmp0"], w=["atmp0"])
            S.op("dve", lambda e: e.tensor_tensor(out=dvn[:], in0=dt1[:, 0:256], in1=lnp[:, 256:512], op=ALU.add),
                 r=["lnp", "atmp0"], w=["dvn"])
            yp = p["yp"]

            def mix(e):
                for g in range(4):
                    i = e.matmul(yp[:, g * 64:(g + 1) * 64], dwsT[:, g, :], dvn[:, g * 64:(g + 1) * 64], start=True,
                                 stop=True, skip_group_check=True)
                return i
            S.op("pe", mix, r=["dwsT", "dvn"], w=["yp"])
            for g in range(4):
                S.op("dve", lambda e, g=g: e.scalar_tensor_tensor(
                    out=dy[:, g * 64:(g + 1) * 64], in0=yp[:, g * 64:(g + 1) * 64], scalar=t["colp"][:, 51 + g:52 + g],
                    in1=dg[:, g * 64:(g + 1) * 64], op0=ALU.add, op1=ALU.mult), r=["colp", "dg"], w=["yp", "dy"])
            S.op("dve", lambda e, j=j: e.tensor_tensor(out=t["ytok"][:, j, :], in0=dy[:], in1=dg[:, 512:768],
                                                       op=ALU.mult), r=["dy", "dg"], w=["ytok"])
        S2, tp, ident = self.S, p["tp"], t["ident"]
        ytd, yTd = t["ytok"], t["yT"]["d"]

        def tr(e):
            for j in range(4):
                for kc in range(2):
                    i = e.transpose(tp[:, (kc * 4 + j) * 128:(kc * 4 + j + 1) * 128], ytd[:, j, kc * 128:(kc + 1) * 128],
                                    ident[:])
            return i
        S.op("pe", tr, r=["ytok", "ident"], w=["tp"])
        S.op("act", lambda e: e.activation(out=yTd[:], in_=tp[:].rearrange("p (k n) -> p k n", k=2), func=AF.Copy),
             r=[], w=["tp", "yTd"])

    def dump_y(self, G):
        S, t, d = self.S, self.t, self.d
        S.op("dve", lambda e: e.nop(), r=[], w=[])


_PROGS = {}
FUSED = False


def get_prog(mode, nl, debug=False):
    key = (mode, nl, debug)
    if key not in _PROGS:
        _PROGS[key] = Prog(mode, nl, debug)
    return _PROGS[key]


def core_inputs(inp, layers, xs, consts, ltabs, extra=None):
    maps = []
    for core in range(8):
        b, hf = core // 2, core % 2
        m = {"x": xs[core], "mem": np.ascontiguousarray(inp["mem"][b])}
        m["w_in"] = np.ascontiguousarray(inp["w_in"][layers])
        m["m_wkv"] = np.ascontiguousarray(inp["m_wkv"][layers])
        m["w_branch"] = np.ascontiguousarray(inp["w_branch"][layers])
        m["w_out"] = np.ascontiguousarray(inp["w_out"][layers])
        for k in ("rowp", "colp", "lamp", "bval", "dwsT", "cw"):
            m[k] = np.stack([ltabs[l][k] for l in layers])
        m["cpack"] = consts[hf]
        if extra is not None:
            m.update(extra[core])
        maps.append(m)
    return maps


def kernel(**inputs):
    inp = {k: np.asarray(v) for k, v in inputs.items()}
    x = inp["x"].astype(np.float32)
    consts = [pack_consts(host_consts(0)), pack_consts(host_consts(1))]
    ltabs = [host_layer_tables(inp, l) for l in range(2)]
    xs = [np.ascontiguousarray(x[c // 2, (c % 2) * NT:(c % 2 + 1) * NT]) for c in range(8)]
    if FUSED:
        fp = get_prog("F", 2)
        res = run_bass_kernel_spmd(fp.nc, core_inputs(inp, [0, 1], xs, consts, ltabs), core_ids=list(range(8)))
        out = np.zeros((4, 4096, D), np.float32)
        for c in range(8):
            out[c // 2, (c % 2) * NT:(c % 2 + 1) * NT] = np.asarray(res.results[c]["xo"], dtype=np.float32)
        return out
    for l in range(2):
        pp = get_prog("P", 1)
        res = run_bass_kernel_spmd(pp.nc, core_inputs(inp, [l], xs, consts, ltabs), core_ids=list(range(8)))
        extra = []
        for c in range(8):
            b = c // 2
            kt = np.stack([res.results[2 * b]["kt_own"], res.results[2 * b + 1]["kt_own"]])
            v = np.stack([res.results[2 * b]["v_own"], res.results[2 * b + 1]["v_own"]])
            extra.append({"kt_all": kt, "v_all": v})
        mg = get_prog("M", 1)
        res = run_bass_kernel_spmd(mg.nc, core_inputs(inp, [l], xs, consts, ltabs, extra), core_ids=list(range(8)))
        xs = [np.asarray(res.results[c]["xo"], dtype=np.float32) for c in range(8)]
    out = np.zeros((4, 4096, D), np.float32)
    for c in range(8):
        out[c // 2, (c % 2) * NT:(c % 2 + 1) * NT] = xs[c]
    return out
```
